# Optimizing a Trainium2 kernel written in Bass

```python
import math
import jax, jax.numpy as jnp
from jax import lax
import numpy as np

D_MODEL = 4096
BATCH = 4
SEQ = 2048
DEPTH = 1

HEAD_DIM = 128
N_DIFF_HEADS = D_MODEL // (4 * HEAD_DIM)
N_DIL_HEADS = D_MODEL // (2 * HEAD_DIM)
DIFF_WIDTH = N_DIFF_HEADS * 2 * HEAD_DIM
DIL_WIDTH = N_DIL_HEADS * HEAD_DIM
MIX_WIDTH = DIFF_WIDTH + DIL_WIDTH
IN_WIDTH = 3 * MIX_WIDTH
DILATED_PAIRS = ((128, 1), (512, 4), (2048, 16))
N_CROSS_HEADS = 4
CROSS_WIDTH = N_CROSS_HEADS * HEAD_DIM
N_MEM = 256
D_FF = 4 * D_MODEL
ROPE_THETA = 10000.0
Q_BLOCK = 128
NORM_EPS = 1e-6
SUBLN_EPS = 1e-5

kernel_name = 'hymba_style_diffattn_dilated_swa_xmem_sqrelu'


def rms_norm(x, g, eps=NORM_EPS):
    xf = x.astype(jnp.float32)
    y = xf * lax.rsqrt(jnp.mean(xf * xf, axis=-1, keepdims=True) + eps)
    return (y * g.astype(jnp.float32)).astype(x.dtype)


def rope_tables(seq_len):
    inv_freq = ROPE_THETA ** (-jnp.arange(0, HEAD_DIM, 2, dtype=jnp.float32) / HEAD_DIM)
    ang = jnp.arange(seq_len, dtype=jnp.float32)[:, None] * inv_freq[None, :]
    return jnp.cos(ang), jnp.sin(ang)


def apply_rope(t, cos, sin):
    tf = t.astype(jnp.float32)
    t1, t2 = jnp.split(tf, 2, axis=-1)
    return jnp.concatenate([t1 * cos - t2 * sin, t2 * cos + t1 * sin], axis=-1).astype(t.dtype)


def diff_attention(q, k, v, lam):
    B, H, _, S, hd = q.shape
    nb = S // Q_BLOCK
    qb = q.reshape(B, H, 2, nb, Q_BLOCK, hd).transpose(3, 0, 1, 2, 4, 5)
    kpos = jnp.arange(S)
    scale = hd ** -0.5

    def one_block(args):
        q_blk, bi = args
        s = jnp.einsum('bhcqd,bhckd->bhcqk', q_blk, k, preferred_element_type=jnp.float32) * scale
        qpos = bi * Q_BLOCK + jnp.arange(Q_BLOCK)
        s = jnp.where(kpos[None, :] <= qpos[:, None], s, -jnp.inf)
        a = jax.nn.softmax(s, axis=-1)
        a = a[:, :, 0] - lam * a[:, :, 1]
        return jnp.einsum('bhqk,bhkd->bhqd', a.astype(v.dtype), v)

    o = lax.map(one_block, (qb, jnp.arange(nb)))
    return o.transpose(1, 2, 0, 3, 4).reshape(B, H, S, 2 * hd)


def banded_causal_attention(q, k, v, window):
    *lead, L, hd = q.shape
    nb = -(-L // Q_BLOCK)
    pad = nb * Q_BLOCK - L
    padw = [(0, 0)] * len(lead) + [(0, pad), (0, 0)]
    q, k, v = jnp.pad(q, padw), jnp.pad(k, padw), jnp.pad(v, padw)
    qb = q.reshape(*lead, nb, Q_BLOCK, hd)
    kb = k.reshape(*lead, nb, Q_BLOCK, hd)
    vb = v.reshape(*lead, nb, Q_BLOCK, hd)
    kk = jnp.concatenate([jnp.zeros_like(kb[..., :1, :, :]), kb[..., :-1, :, :]], axis=-3)
    vv = jnp.concatenate([jnp.zeros_like(vb[..., :1, :, :]), vb[..., :-1, :, :]], axis=-3)
    kk = jnp.concatenate([kk, kb], axis=-2)
    vv = jnp.concatenate([vv, vb], axis=-2)
    s = jnp.einsum('...nqd,...nkd->...nqk', qb, kk, preferred_element_type=jnp.float32) * (hd ** -0.5)
    p_idx = jnp.arange(Q_BLOCK)[:, None]
    c_idx = jnp.arange(2 * Q_BLOCK)[None, :]
    dist = p_idx + Q_BLOCK - c_idx
    first = (jnp.arange(nb) == 0)[:, None, None]
    mask = (dist >= 0) & (dist <= window) & ~(first & (c_idx < Q_BLOCK))
    s = jnp.where(mask, s, -jnp.inf)
    m = jnp.max(s, axis=-1, keepdims=True)
    p = jnp.exp(s - m)
    l = jnp.sum(p, axis=-1, keepdims=True)
    o = jnp.einsum('...nqk,...nkd->...nqd', (p / l).astype(v.dtype), vv)
    lse = (m + jnp.log(l))[..., 0]
    o = o.reshape(*lead, nb * Q_BLOCK, hd)[..., :L, :]
    lse = lse.reshape(*lead, nb * Q_BLOCK)[..., :L]
    return o, lse


def dilated_branch(q, k, v, window, dilation):
    B, H, S, hd = q.shape
    L = S // dilation
    def by_stride(t):
        return t.reshape(B, H, L, dilation, hd).transpose(0, 1, 3, 2, 4)
    o, lse = banded_causal_attention(by_stride(q), by_stride(k), by_stride(v), window // dilation)
    o = o.transpose(0, 1, 3, 2, 4).reshape(B, H, S, hd)
    lse = lse.transpose(0, 1, 3, 2).reshape(B, H, S)
    return o, lse


def dilated_attention(q, k, v):
    outs, lses = [], []
    for window, dilation in DILATED_PAIRS:
        o, lse = dilated_branch(q, k, v, window, dilation)
        outs.append(o.astype(jnp.float32))
        lses.append(lse)
    w = jax.nn.softmax(jnp.stack(lses), axis=0)
    return jnp.einsum('rbhs,rbhsd->bhsd', w, jnp.stack(outs)).astype(q.dtype)


def setup_inputs(seed: int = 0) -> dict:
    key = jax.random.key(seed)
    ks = jax.random.split(key, 18)
    f32 = jnp.float32
    def nrm(k, shape, fan_in):
        return jax.random.normal(k, shape, f32) * (fan_in ** -0.5)
    def gain(k, shape):
        return 1.0 + 0.02 * jax.random.normal(k, shape, f32)
    return {
        'x': jax.random.normal(ks[0], (BATCH, SEQ, D_MODEL), f32),
        'mem': jax.random.normal(ks[1], (BATCH, N_MEM, D_MODEL), f32),
        'norm_mix': gain(ks[2], (DEPTH, D_MODEL)),
        'w_in': nrm(ks[3], (DEPTH, D_MODEL, IN_WIDTH), D_MODEL),
        'diff_lambda': 0.1 * jax.random.normal(ks[4], (DEPTH, 4, HEAD_DIM), f32),
        'diff_subln': gain(ks[5], (DEPTH, 2 * HEAD_DIM)),
        'w_out': nrm(ks[6], (DEPTH, MIX_WIDTH, D_MODEL), MIX_WIDTH),
        'norm_cross': gain(ks[7], (DEPTH, D_MODEL)),
        'norm_mem': gain(ks[8], (DEPTH, D_MODEL)),
        'w_cq': nrm(ks[9], (DEPTH, D_MODEL, CROSS_WIDTH), D_MODEL),
        'w_ckv': nrm(ks[10], (DEPTH, D_MODEL, 2 * CROSS_WIDTH), D_MODEL),
        'w_co': nrm(ks[11], (DEPTH, CROSS_WIDTH, D_MODEL), CROSS_WIDTH),
        'norm_mlp': gain(ks[12], (DEPTH, D_MODEL)),
        'w_up': nrm(ks[13], (DEPTH, D_MODEL, D_FF), D_MODEL),
        'w_down': nrm(ks[14], (DEPTH, D_FF, D_MODEL), D_FF),
        'norm_final': gain(ks[15], (D_MODEL,)),
    }


def reference(x, mem, norm_mix, w_in, diff_lambda, diff_subln, w_out, norm_cross, norm_mem,
              w_cq, w_ckv, w_co, norm_mlp, w_up, w_down, norm_final):
    B, S, _ = x.shape
    M = mem.shape[1]
    cos, sin = rope_tables(S)
    split_at = [DIFF_WIDTH, 2 * DIFF_WIDTH, 3 * DIFF_WIDTH,
                3 * DIFF_WIDTH + DIL_WIDTH, 3 * DIFF_WIDTH + 2 * DIL_WIDTH]
    for i in range(DEPTH):
        h = rms_norm(x, norm_mix[i])
        proj = h @ w_in[i]
        dq, dk, dv, sq, sk, sv = jnp.split(proj, split_at, axis=-1)

        lambda_init = 0.8 - 0.6 * math.exp(-0.3 * i)
        lp = diff_lambda[i].astype(jnp.float32)
        lam = (jnp.exp(jnp.sum(lp[0] * lp[1])) - jnp.exp(jnp.sum(lp[2] * lp[3])) + lambda_init)
        dq = apply_rope(dq.reshape(B, S, N_DIFF_HEADS, 2, HEAD_DIM).transpose(0, 2, 3, 1, 4), cos, sin)
        dk = apply_rope(dk.reshape(B, S, N_DIFF_HEADS, 2, HEAD_DIM).transpose(0, 2, 3, 1, 4), cos, sin)
        dv = dv.reshape(B, S, N_DIFF_HEADS, 2 * HEAD_DIM).transpose(0, 2, 1, 3)
        d_out = diff_attention(dq, dk, dv, lam)
        d_out = rms_norm(d_out, diff_subln[i], SUBLN_EPS) * (1.0 - lambda_init)
        d_out = d_out.transpose(0, 2, 1, 3).reshape(B, S, DIFF_WIDTH)

        sq = apply_rope(sq.reshape(B, S, N_DIL_HEADS, HEAD_DIM).transpose(0, 2, 1, 3), cos, sin)
        sk = apply_rope(sk.reshape(B, S, N_DIL_HEADS, HEAD_DIM).transpose(0, 2, 1, 3), cos, sin)
        sv = sv.reshape(B, S, N_DIL_HEADS, HEAD_DIM).transpose(0, 2, 1, 3)
        s_out = dilated_attention(sq, sk, sv)
        s_out = s_out.transpose(0, 2, 1, 3).reshape(B, S, DIL_WIDTH)

        x = x + jnp.concatenate([d_out, s_out], axis=-1) @ w_out[i]

        hc = rms_norm(x, norm_cross[i])
        mn = rms_norm(mem, norm_mem[i])
        cq = (hc @ w_cq[i]).reshape(B, S, N_CROSS_HEADS, HEAD_DIM)
        ck, cv = jnp.split((mn @ w_ckv[i]).reshape(B, M, 2, N_CROSS_HEADS, HEAD_DIM), 2, axis=2)
        ck, cv = ck[:, :, 0], cv[:, :, 0]
        cs = jnp.einsum('bshd,bmhd->bhsm', cq, ck, preferred_element_type=jnp.float32) * (HEAD_DIM ** -0.5)
        ca = jax.nn.softmax(cs, axis=-1).astype(cv.dtype)
        co = jnp.einsum('bhsm,bmhd->bshd', ca, cv).reshape(B, S, CROSS_WIDTH)
        x = x + co @ w_co[i]

        hm = rms_norm(x, norm_mlp[i])
        x = x + jnp.square(jax.nn.relu(hm @ w_up[i])) @ w_down[i]
    return rms_norm(x, norm_final)
```

```python
import math
from contextlib import ExitStack

import numpy as np
import ml_dtypes
import concourse.bass as bass
import concourse.mybir as mybir
from concourse.bass_utils import run_bass_kernel_spmd

F32 = mybir.dt.float32
BF16 = mybir.dt.bfloat16
AF = mybir.ActivationFunctionType
ALU = mybir.AluOpType
PE, ACT, DVE, POOL, SP = "pe", "act", "dve", "pool", "sp"

S = 2048
HALF = 1024
TT = 512
HD = 128
NMEM = 256
NCH = 4
LAMBDA_INIT = 0.8 - 0.6 * math.exp(-0.3 * 0)
NEG = -30000.0


class T:
    __slots__ = ("name", "writer", "readers", "sem", "dma_cnt")

    def __init__(self, name):
        self.name = name
        self.writer = None
        self.readers = []
        self.sem = None
        self.dma_cnt = 0


class Op:
    __slots__ = ("eng", "fn", "deps", "is_dma", "dst", "needs_inc", "tok")

    def __init__(self, eng, fn, is_dma=False, dst=None):
        self.eng = eng
        self.fn = fn
        self.deps = []
        self.is_dma = is_dma
        self.dst = dst
        self.needs_inc = False
        self.tok = None


class Prog:
    def __init__(self):
        self.ops = {PE: [], ACT: [], DVE: [], POOL: [], SP: []}
        self.dma_tiles = []
        self.last_dma = {}
        self.fence_ap = None

    def _link(self, op, deps):
        seen = set()
        for d in deps:
            if d is op or id(d) in seen:
                continue
            seen.add(id(d))
            if d.eng == PE and op.eng == PE and not d.is_dma and not op.is_dma:
                continue
            op.deps.append(d)
            if not d.is_dma:
                d.needs_inc = True

    def _add(self, op, reads, writes):
        deps = []
        for t in reads:
            if t.writer is not None:
                deps.append(t.writer)
        for t in writes:
            if t.writer is not None:
                deps.append(t.writer)
            deps.extend(t.readers)
        self._link(op, deps)
        for t in writes:
            t.writer = op
            t.readers = []
        for t in reads:
            if t.writer is not op:
                t.readers.append(op)
        self.ops[op.eng].append(op)
        return op

    def op(self, eng, fn, reads=(), writes=()):
        return self._add(Op(eng, fn), list(reads), list(writes))

    def dma(self, eng, fn, dst, reads=(), writes=()):
        if dst.sem is None:
            dst.sem = -1
            self.dma_tiles.append(dst)
        o = Op(eng, fn, is_dma=True, dst=dst)
        self._add(o, list(reads), list(writes))
        self.last_dma[id(dst)] = o
        return o

    def fence(self):
        deps = []
        for e in (PE, ACT, DVE, POOL):
            for o in reversed(self.ops[e]):
                if not o.is_dma and o.fn is not None:
                    deps.append(o)
                    break
        deps.extend(self.last_dma.values())
        self.last_dma = {}
        ap = self.fence_ap
        j = Op(POOL, lambda e: e.memset(ap, 0.0))
        self._link(j, deps)
        self.ops[POOL].append(j)
        j.needs_inc = True
        for e in (PE, ACT, DVE, SP):
            w = Op(e, None)
            w.deps.append(j)
            self.ops[e].append(w)

    def emit(self, nc):
        with ExitStack() as es:
            esem = {}
            for e in (PE, ACT, DVE, POOL):
                esem[e] = es.enter_context(nc.semaphore("s_" + e))
            for i, t in enumerate(self.dma_tiles):
                t.sem = es.enter_context(nc.semaphore("d%d" % i))
            for e, lst in self.ops.items():
                cnt = 0
                for o in lst:
                    if o.is_dma:
                        o.dst.dma_cnt += 1
                        o.tok = (o.dst.sem, 16 * o.dst.dma_cnt)
                    elif o.needs_inc:
                        cnt += 1
                        o.tok = (esem[e], cnt)
            block = es.enter_context(nc.Block())
            handles = {PE: block.tensor, ACT: block.scalar, DVE: block.vector,
                       POOL: block.gpsimd, SP: block.sync}
            for e, lst in self.ops.items():
                if not lst:
                    continue

                def body(eng, lst=lst):
                    known = {}
                    for o in lst:
                        for d in o.deps:
                            sem, val = d.tok
                            k = id(sem)
                            if known.get(k, 0) >= val:
                                continue
                            known[k] = val
                            eng.wait_ge(sem, val)
                        if o.fn is None:
                            continue
                        ins = o.fn(eng)
                        if o.tok is not None:
                            ins.then_inc(o.tok[0], 16 if o.is_dma else 1)
                handles[e](body)


class Ring:
    def __init__(self, items):
        self.items = items
        self.i = 0

    def next(self):
        it = self.items[self.i % len(self.items)]
        self.i += 1
        return it


def MM(out, lhsT, rhs, start=True, stop=True):
    return lambda e: e.matmul(out, lhsT=lhsT, rhs=rhs, start=start, stop=stop)


def TR(out, in_, ident):
    return lambda e: e.transpose(out, in_, ident)


def AC(out, in_, func, bias=None, scale=None):
    kw = {}
    if bias is not None:
        kw["bias"] = bias
    if scale is not None:
        kw["scale"] = scale
    return lambda e: e.activation(out=out, in_=in_, func=func, **kw)


def TTO(out, in0, in1, op):
    return lambda e: e.tensor_tensor(out=out, in0=in0, in1=in1, op=op)


def STT(out, in0, scalar, in1, op0, op1):
    return lambda e: e.scalar_tensor_tensor(out=out, in0=in0, scalar=scalar, in1=in1, op0=op0, op1=op1)


def TS(out, in0, s1, op0, op1=None, accum_out=None):
    kw = {}
    if op1 is not None:
        kw["op1"] = op1
    if accum_out is not None:
        kw["accum_out"] = accum_out
    return lambda e: e.tensor_scalar(out=out, in0=in0, scalar1=s1, scalar2=None, op0=op0, **kw)


def RC(out, in_):
    return lambda e: e.reciprocal(out=out, in_=in_)


def CP(out, in_):
    return lambda e: e.tensor_copy(out=out, in_=in_)


def MS(ap, v):
    return lambda e: e.memset(ap, v)


def DM(out, in_, cast=False):
    if cast:
        return lambda e: e.dma_start(out=out, in_=in_, max_dma_last_dim=8192)
    return lambda e: e.dma_start(out=out, in_=in_)


def build_program(D, stop=99):
    ND = D // 128
    NQ = D // 256
    HDIFF = D // 512
    HDIL = D // 256
    DFF = 4 * D
    NFF = DFF // 128
    KS = min(8, ND)
    NSLAB = NFF // KS
    OG = min(512, D)
    NOG = D // OG
    GC = min(1024, D)
    NGC = D // GC
    CVK = min(8, ND)
    NCV = ND // CVK
    SLABE = 4096
    assert ND * 128 <= SLABE and KS * OG <= SLABE and NCH * GC <= SLABE and CVK * 512 <= SLABE

    nc = bass.Bass("TRN2", target_bir_lowering=False)

    def din(name, shape, dt=F32):
        return nc.dram_tensor(name, list(shape), dt, kind="ExternalInput").ap()

    xT_own = din("xT_own", [ND, 128, HALF])
    xT_oth = din("xT_oth", [ND, 128, HALF])
    memT = din("memT", [ND, 128, NMEM])
    cos_own = din("cos_own", [128, HALF])
    sin_own = din("sin_own", [128, HALF])
    cos_oth = din("cos_oth", [128, HALF])
    sin_oth = din("sin_oth", [128, HALF])
    visb_d = din("visb", [128, 1])
    cb_d = din("cb", [128, 23 * 128], BF16)
    gv_d = din("gv", [128, 5 * ND + 2])
    lp_d = din("lp", [128, 4 * 128])
    w_in_r = din("w_in_r", [3 * D // 128, 128, ND * 128])
    w_out_r = din("w_out_r", [ND, 128, ND * 128])
    w_cq_r = din("w_cq_r", [NCH, 128, ND * 128])
    w_ck_r = din("w_ck_r", [NCH, 128, ND * 128])
    w_cv_r = din("w_cv_r", [NCV, 128, CVK * 512])
    w_co_r = din("w_co_r", [NGC, 128, NCH * GC])
    w_up_r = din("w_up_r", [NFF, 128, ND * 128])
    w_dn_r = din("w_dn_r", [NSLAB * NOG, 128, KS * OG])
    yT = nc.dram_tensor("yT", [ND, 128, HALF], F32, kind="ExternalOutput").ap()
    qT_s = nc.dram_tensor("qT_s", [2 * NQ, 128, HALF], BF16).ap()
    kT_s = nc.dram_tensor("kT_s", [2 * NQ, 128, S], BF16).ap()
    v_s = nc.dram_tensor("v_s", [2 * NQ, 128, 16 * 128], BF16).ap()
    at_s = nc.dram_tensor("at_s", [ND, 128, HALF], BF16).ap()

    NB_ARENA = max(ND * HALF + 4 * SLABE + 7 * TT, ND * TT + 4 * SLABE + 2 * KS * TT + 8 * TT + 2048 + 5 * TT, 26624)
    NF_ARENA = max(128 + ND * TT + 6 * TT, 128 + 2 * HALF + 9 * TT)

    with ExitStack() as es:
        AB = es.enter_context(nc.sbuf_tensor("arena_b", [128, NB_ARENA], BF16))
        AFp = es.enter_context(nc.sbuf_tensor("arena_f", [128, NF_ARENA], F32))
        CB = es.enter_context(nc.sbuf_tensor("cb_sb", [128, 23 * 128], BF16))
        GV = es.enter_context(nc.sbuf_tensor("gv_sb", [128, 5 * ND + 2], F32))
        SM = es.enter_context(nc.sbuf_tensor("small", [128, 16], F32))
        LP = es.enter_context(nc.sbuf_tensor("lp_sb", [128, 4 * 128], F32))
        banks = [es.enter_context(nc.psum_tensor("bank%d" % i, [128, 512], F32)) for i in range(7)]
        bankT = es.enter_context(nc.psum_tensor("bankT", [128, 1024], BF16))
        banks.append(None)
        tbank = [T("bank%d" % i) for i in range(8)]

        P = Prog()
        P.fence_ap = SM[:, 15:16]

        Mmask = CB[:, 0:19 * 128]
        TRI = CB[:, 19 * 128:20 * 128]
        PERM = CB[:, 20 * 128:21 * 128]
        ONES = CB[:, 21 * 128:22 * 128]
        IDENT = CB[:, 22 * 128:23 * 128]
        t_cb, t_gv, t_lp = T("cb"), T("gv"), T("lp")
        G_MIX, G_CROSS, G_MEM, G_MLP, G_FIN, G_SUB = 0, ND, 2 * ND, 3 * ND, 4 * ND, 5 * ND
        visb = SM[:, 0:1]
        neglam = SM[:, 1:2]
        eps6 = SM[:, 7:8]
        eps5 = SM[:, 8:9]
        gsub = SM[:, 9:11]

        P.dma(SP, DM(CB[:], cb_d), t_cb, writes=[t_cb])
        P.dma(SP, DM(GV[:], gv_d), t_gv, writes=[t_gv])
        P.dma(SP, DM(LP[:], lp_d), t_lp, writes=[t_lp])
        t_vis = T("vis")
        P.dma(SP, DM(visb, visb_d), t_vis, writes=[t_vis])
        t_s1, t_s2, t_e, t_d, t_nl, t_eps, t_gs = T("s1"), T("s2"), T("e"), T("d"), T("nl"), T("eps"), T("gs")
        junk = AFp[:, 0:128]
        t_junk = T("junk")
        P.op(DVE, TTO(junk, LP[:, 0:128], LP[:, 128:256], ALU.mult), reads=[t_lp], writes=[t_junk])
        P.op(DVE, TS(junk, junk, 1.0, ALU.mult, op1=ALU.add, accum_out=SM[:, 2:3]), reads=[t_junk], writes=[t_junk, t_s1])
        P.op(DVE, TTO(junk, LP[:, 256:384], LP[:, 384:512], ALU.mult), reads=[t_lp], writes=[t_junk])
        P.op(DVE, TS(junk, junk, 1.0, ALU.mult, op1=ALU.add, accum_out=SM[:, 3:4]), reads=[t_junk], writes=[t_junk, t_s2])
        P.op(ACT, AC(SM[:, 4:6], SM[:, 2:4], AF.Exp), reads=[t_s1, t_s2], writes=[t_e])
        P.op(DVE, TTO(SM[:, 6:7], SM[:, 5:6], SM[:, 4:5], ALU.subtract), reads=[t_e], writes=[t_d])
        P.op(DVE, TS(neglam, SM[:, 6:7], -LAMBDA_INIT, ALU.add), reads=[t_d], writes=[t_nl])
        P.op(DVE, MS(eps6, 1e-6), writes=[t_eps])
        P.op(DVE, MS(eps5, 1e-5), writes=[t_eps])
        P.op(DVE, TS(gsub, GV[:, G_SUB:G_SUB + 2], 1.0 - LAMBDA_INIT, ALU.mult), reads=[t_gv], writes=[t_gs])

        def carve(base, n):
            return base, base + n

        def rstd_from_bank(bk_ap, tb_, rt, t_rt_, inv_n, eps_ap):
            P.op(ACT, AC(rt, bk_ap, AF.Sqrt, bias=eps_ap, scale=inv_n), reads=[tb_, t_eps], writes=[t_rt_])
            P.op(DVE, RC(rt, rt), reads=[t_rt_], writes=[t_rt_])

        def run_slabs(slots, srcs, body):
            look = len(slots) - 1
            q = []
            nxt = 0
            for i in range(len(srcs)):
                while nxt < len(srcs) and nxt <= i + look:
                    ap, t = slots[run_slabs.n % len(slots)]
                    run_slabs.n += 1
                    src_ap, nel = srcs[nxt]
                    P.dma(POOL, DM(ap[:, 0:nel], src_ap, cast=True), t, writes=[t])
                    q.append((ap, t))
                    nxt += 1
                ap, t = q.pop(0)
                body(i, ap, t)
        run_slabs.n = 0

        def build_hT(xsrc, dst, t_dst, g_off, ntile, width, xs_ring, sq_ring, rt, t_rt_, rbank):
            for t in range(ntile):
                for c in range(ND):
                    xs, t_xs = xs_ring.next()
                    sq, t_sq = sq_ring.next()
                    P.dma(SP, DM(xs[:, 0:width], xsrc[c][:, t * width:(t + 1) * width]), t_xs, writes=[t_xs])
                    P.op(ACT, AC(sq[:, 0:width], xs[:, 0:width], AF.Square), reads=[t_xs], writes=[t_sq])
                    P.op(PE, MM(banks[rbank][:, 0:width], ONES, sq[:, 0:width], start=(c == 0), stop=(c == ND - 1)),
                         reads=[t_sq, t_cb], writes=[tbank[rbank]])
                rstd_from_bank(banks[rbank][:, 0:width], tbank[rbank], rt[:, 0:width], t_rt_, 1.0 / D, eps6)
                for c in range(ND):
                    xs, t_xs = xs_ring.next()
                    P.dma(SP, DM(xs[:, 0:width], xsrc[c][:, t * width:(t + 1) * width]), t_xs, writes=[t_xs])
                    P.op(DVE, STT(dst[:, c, t * width:(t + 1) * width], xs[:, 0:width], GV[:, g_off + c:g_off + c + 1],
                                  rt[:, 0:width], ALU.mult, ALU.mult),
                         reads=[t_xs, t_rt_, t_gv], writes=[t_dst[c][t]])

        if stop == 0:
            P.fence()
            P.emit(nc)
            return nc
        o = 0
        hT0, o = carve(o, ND * HALF)
        slab0, o = carve(o, 4 * SLABE)
        sq0, o = carve(o, 2 * TT)
        tb0, o = carve(o, 2 * TT)
        st0, o = carve(o, 3 * TT)
        assert o <= NB_ARENA, (o, NB_ARENA)
        hT = AB[:, hT0:hT0 + ND * HALF].rearrange("p (c t) -> p c t", c=ND)
        t_h = [[T("h%d_%d" % (c, t)) for t in range(2)] for c in range(ND)]
        slabsA = [(AB[:, slab0 + i * SLABE: slab0 + (i + 1) * SLABE], T("slab%d" % i)) for i in range(4)]
        sq_rA = Ring([(AB[:, sq0 + i * TT: sq0 + (i + 1) * TT], T("sq%d" % i)) for i in range(2)])
        tb_r = Ring([(AB[:, tb0 + i * TT: tb0 + (i + 1) * TT], T("tb%d" % i)) for i in range(2)])
        st_r = Ring([(AB[:, st0 + i * TT: st0 + (i + 1) * TT], T("st%d" % i)) for i in range(3)])
        f = 128
        cosT, f = carve(f, HALF)
        sinT, f = carve(f, HALF)
        xs0, f = carve(f, 4 * TT)
        rt0, f = carve(f, TT)
        t10, f = carve(f, 2 * TT)
        t20, f = carve(f, 2 * TT)
        assert f <= NF_ARENA, (f, NF_ARENA)
        cos_sb = AFp[:, cosT:cosT + HALF]
        sin_sb = AFp[:, sinT:sinT + HALF]
        t_cos, t_sin = T("cos"), T("sin")
        xs_rA = Ring([(AFp[:, xs0 + i * TT: xs0 + (i + 1) * TT], T("xs%d" % i)) for i in range(4)])
        rtA, t_rtA = AFp[:, rt0:rt0 + TT], T("rt")
        t1_r = Ring([(AFp[:, t10 + i * TT: t10 + (i + 1) * TT], T("t1_%d" % i)) for i in range(2)])
        t2_r = Ring([(AFp[:, t20 + i * TT: t20 + (i + 1) * TT], T("t2_%d" % i)) for i in range(2)])
        proj_r = Ring([0, 1, 2, 6])
        RB = 3
        pp_r = Ring([4, 5])
        tr_r = Ring([7])
        t_q_s = [T("q_s%d" % i) for i in range(2 * NQ)]
        t_k_s = [T("k_s%d" % i) for i in range(2 * NQ)]
        t_v_s = [T("v_s%d" % i) for i in range(2 * NQ)]
        t_at_s = [T("at_s%d" % i) for i in range(ND)]

        def chunk_kind(cc):
            r = cc // NQ
            j = cc % NQ
            return ("q", "k", "v", "q", "k", "v")[r], (j if r < 3 else NQ + j)

        for pas in (0, 1):
            own = (pas == 0)
            xsrc = xT_own if own else xT_oth
            P.dma(SP, DM(cos_sb, cos_own if own else cos_oth), t_cos, writes=[t_cos])
            P.dma(SP, DM(sin_sb, sin_own if own else sin_oth), t_sin, writes=[t_sin])
            build_hT(xsrc, hT, t_h, G_MIX, 2, TT, xs_rA, sq_rA, rtA, t_rtA, RB)
            chunks = [cc for cc in range(3 * D // 128) if own or chunk_kind(cc)[0] != "q"]
            kvoff = HALF if own else 0

            pending = []

            def flush():
                while pending:
                    pending.pop(0)()

            def body(i, slab, t_slab, chunks=chunks, own=own, kvoff=kvoff):
                cc = chunks[i]
                kind, idx = chunk_kind(cc)
                for t in range(2):
                    b = proj_r.next()
                    for c in range(ND):
                        P.op(PE, MM(banks[b][:, :], slab[:, c * 128:(c + 1) * 128], hT[:, c, t * TT:(t + 1) * TT],
                                    start=(c == 0), stop=(c == ND - 1)),
                             reads=[t_slab, t_h[c][t]], writes=[tbank[b]])
                    flush()
                    tb, t_tb = tb_r.next()
                    P.op(ACT, AC(tb, banks[b][:, :], AF.Copy), reads=[tbank[b]], writes=[t_tb])
                    pending.append(lambda b=b, tb=tb, t_tb=t_tb, t=t, kind=kind, idx=idx: post(b, tb, t_tb, t, kind, idx))

            def post(b, tb, t_tb, t, kind, idx, own=own, kvoff=kvoff):
                if True:
                    st, t_st = st_r.next()
                    if kind == "v":
                        tr = tr_r.next()
                        trb = bankT
                        for j in range(4):
                            P.op(PE, TR(trb[:, j * 128:(j + 1) * 128], tb[:, j * 128:(j + 1) * 128], IDENT),
                                 reads=[t_tb, t_cb], writes=[tbank[tr]])
                        P.op(DVE, CP(st, trb[:, 0:512]), reads=[tbank[tr]], writes=[t_st])
                        blk0 = (8 if own else 0) + t * 4
                        P.dma(SP, DM(v_s[idx][:, blk0 * 128:(blk0 + 4) * 128], st), t_st,
                              reads=[t_st], writes=[t_v_s[idx]])
                    else:
                        pp = pp_r.next()
                        P.op(PE, MM(banks[pp][:, :], PERM, tb), reads=[t_tb, t_cb], writes=[tbank[pp]])
                        t1, t_t1 = t1_r.next()
                        t2, t_t2 = t2_r.next()
                        P.op(DVE, TTO(t1, banks[b][:, :], cos_sb[:, t * TT:(t + 1) * TT], ALU.mult),
                             reads=[tbank[b], t_cos, t_tb], writes=[t_t1])
                        P.op(DVE, TTO(t2, banks[pp][:, :], sin_sb[:, t * TT:(t + 1) * TT], ALU.mult),
                             reads=[tbank[pp], t_sin], writes=[t_t2])
                        P.op(DVE, TTO(st, t1, t2, ALU.add), reads=[t_t1, t_t2], writes=[t_st])
                        if kind == "q":
                            P.dma(SP, DM(qT_s[idx][:, t * TT:(t + 1) * TT], st), t_st, reads=[t_st], writes=[t_q_s[idx]])
                        else:
                            P.dma(SP, DM(kT_s[idx][:, kvoff + t * TT: kvoff + (t + 1) * TT], st), t_st,
                                  reads=[t_st], writes=[t_k_s[idx]])

            run_slabs(slabsA, [(w_in_r[cc], ND * 128) for cc in chunks], body)
            flush()

        if stop == 1:
            P.fence()
            P.emit(nc)
            return nc
        P.fence()
        HBE = 2 * HALF + 2 * S + 2 * 16 * 128
        o = 0
        hb0, o = carve(o, 2 * HBE)
        p0, o = carve(o, 8 * TT)
        as0, o = carve(o, 2 * TT)
        sqd0, o = carve(o, 2 * TT)
        assert o <= NB_ARENA, (o, NB_ARENA)
        hbufs = []
        for i in range(2):
            base = hb0 + i * HBE
            q_ap = AB[:, base: base + 2 * HALF].rearrange("p (c t) -> p c t", c=2)
            k_ap = AB[:, base + 2 * HALF: base + 2 * HALF + 2 * S].rearrange("p (c t) -> p c t", c=2)
            v_ap = AB[:, base + 2 * HALF + 2 * S: base + HBE].rearrange("p (c b d) -> p c b d", c=2, b=16)
            hbufs.append((q_ap, k_ap, v_ap, [T("hq%d_%d" % (i, c)) for c in range(2)],
                          [T("hk%d_%d" % (i, c)) for c in range(2)], [T("hv%d_%d" % (i, c)) for c in range(2)]))
        p_r = Ring([(AB[:, p0 + i * TT: p0 + (i + 1) * TT], T("p%d" % i)) for i in range(8)])
        as_r = Ring([(AB[:, as0 + i * TT: as0 + (i + 1) * TT], T("as%d" % i)) for i in range(2)])
        sqd = [(AB[:, sqd0 + i * TT: sqd0 + (i + 1) * TT], T("sqd%d" % i)) for i in range(2)]
        f = 128
        rl0, f = carve(f, TT)
        on0, f = carve(f, 2 * TT)
        dd0, f = carve(f, 2 * TT)
        o2_0, f = carve(f, TT)
        rs0, f = carve(f, TT)
        osb0, f = carve(f, 2 * TT)
        lsb0, f = carve(f, TT)
        assert f <= NF_ARENA, (f, NF_ARENA)
        lsb, t_lsb = AFp[:, lsb0:lsb0 + TT], T("lsb")
        osb = [(AFp[:, osb0 + i * TT: osb0 + (i + 1) * TT], T("osb%d" % i)) for i in range(2)]
        rlA, t_rlA = AFp[:, rl0:rl0 + TT], T("rl")
        on = [(AFp[:, on0 + i * TT: on0 + (i + 1) * TT], T("on%d" % i)) for i in range(2)]
        dd = [(AFp[:, dd0 + i * TT: dd0 + (i + 1) * TT], T("dd%d" % i)) for i in range(2)]
        o2_ap, t_o2 = AFp[:, o2_0:o2_0 + TT], T("o2")
        rs_ap, t_rs = AFp[:, rs0:rs0 + TT], T("rs")
        OB = [0, 1]
        LB = 2
        s_r_diff = Ring([3, 4, 5])
        s_r_dil = Ring([1, 3, 4, 5])
        scale = 1.0 / math.sqrt(HD)

        jobs = [("diff", h) for h in range(HDIFF)] + [("dil", h) for h in range(HDIL)]

        def load_head(ji):
            kind, h = jobs[ji]
            q_ap, k_ap, v_ap, tq, tk, tv = hbufs[ji % 2]
            ncomp = 2 if kind == "diff" else 1
            for c in range(ncomp):
                qi = (2 * h + c) if kind == "diff" else (NQ + h)
                P.dma(SP, DM(q_ap[:, c, :], qT_s[qi]), tq[c], reads=[t_q_s[qi]], writes=[tq[c]])
                P.dma(SP, DM(k_ap[:, c, :], kT_s[qi]), tk[c], reads=[t_k_s[qi]], writes=[tk[c]])
                P.dma(SP, DM(v_ap[:, c, :, :], v_s[qi].rearrange("p (b d) -> p b d", b=16)), tv[c],
                      reads=[t_v_s[qi]], writes=[tv[c]])

        tail_pending = []

        def attend(kind, h, q_ap, k_ap, v_ap, tq, tk, tv):
            ncomp = 2 if kind == "diff" else 1
            nvo = ncomp
            s_r = s_r_diff if kind == "diff" else s_r_dil
            LA = len(s_r.items) - 1
            for qt in range(2):
                qb0 = 8 + qt * 4
                blocks = list(range(0, qb0 + 4))
                for comp in range(ncomp):
                    sb_of = {}

                    def qk(kb):
                        c0 = max(0, kb - qb0) * 128
                        sb = s_r.next()
                        sb_of[kb] = sb
                        P.op(PE, MM(banks[sb][:, c0:TT], k_ap[:, comp, kb * 128:(kb + 1) * 128],
                                    q_ap[:, comp, qt * TT + c0:(qt + 1) * TT]),
                             reads=[tk[comp], tq[comp]], writes=[tbank[sb]])

                    for j in range(min(LA, len(blocks))):
                        qk(blocks[j])
                    for i, kb in enumerate(blocks):
                        if i == 8:
                            while tail_pending:
                                tail_pending.pop(0)()
                        if i + LA < len(blocks):
                            qk(blocks[i + LA])
                        c0 = max(0, kb - qb0) * 128
                        sb = sb_of[kb]
                        pt, t_pt = p_r.next()
                        if kb < 8:
                            P.op(ACT, AC(pt[:, c0:TT], banks[sb][:, c0:TT], AF.Exp, bias=visb, scale=scale),
                                 reads=[tbank[sb], t_vis], writes=[t_pt])
                        else:
                            P.op(ACT, AC(pt[:, c0:TT], banks[sb][:, c0:TT], AF.Exp, scale=scale),
                                 reads=[tbank[sb]], writes=[t_pt])
                        if kind == "dil":
                            mo = (qb0 - kb + 3) * 128 + c0
                            P.op(DVE, TTO(pt[:, c0:TT], pt[:, c0:TT], Mmask[:, mo:mo + TT - c0], ALU.mult),
                                 reads=[t_pt, t_cb], writes=[t_pt])
                        elif kb >= qb0:
                            P.op(DVE, TTO(pt[:, c0:c0 + 128], pt[:, c0:c0 + 128], TRI, ALU.mult),
                                 reads=[t_pt, t_cb], writes=[t_pt])
                        first = (i == 0)
                        last = (i == len(blocks) - 1)
                        for oc in range(nvo):
                            P.op(PE, MM(banks[OB[oc]][:, c0:TT], v_ap[:, oc, kb, :], pt[:, c0:TT], start=first, stop=last),
                                 reads=[tv[oc], t_pt], writes=[tbank[OB[oc]]])
                        P.op(PE, MM(banks[LB][:, c0:TT], ONES, pt[:, c0:TT], start=first, stop=last),
                             reads=[t_pt, t_cb], writes=[tbank[LB]])
                    P.op(DVE, CP(lsb, banks[LB][:, :]), reads=[tbank[LB]], writes=[t_lsb])
                    P.op(ACT, AC(osb[0][0], banks[OB[0]][:, :], AF.Copy), reads=[tbank[OB[0]]], writes=[osb[0][1]])
                    if nvo == 2:
                        P.op(DVE, CP(osb[1][0], banks[OB[1]][:, :]), reads=[tbank[OB[1]]], writes=[osb[1][1]])
                    P.op(DVE, RC(rlA, lsb), reads=[t_lsb], writes=[t_rlA])
                    if kind == "dil":
                        st, t_st = as_r.next()
                        P.op(DVE, TTO(st, osb[0][0], rlA, ALU.mult), reads=[osb[0][1], t_rlA], writes=[t_st])
                        ch = D // 256 + h
                        P.dma(SP, DM(at_s[ch][:, qt * TT:(qt + 1) * TT], st), t_st, reads=[t_st], writes=[t_at_s[ch]])
                    elif comp == 0:
                        for oc in range(2):
                            P.op(DVE, TTO(on[oc][0], osb[oc][0], rlA, ALU.mult),
                                 reads=[osb[oc][1], t_rlA], writes=[on[oc][1]])
                    else:
                        for oc in range(2):
                            P.op(DVE, TTO(o2_ap, osb[oc][0], rlA, ALU.mult),
                                 reads=[osb[oc][1], t_rlA], writes=[t_o2])
                            P.op(DVE, STT(dd[oc][0], o2_ap, neglam, on[oc][0], ALU.mult, ALU.add),
                                 reads=[t_o2, t_nl, on[oc][1]], writes=[dd[oc][1]])
                            P.op(DVE, TTO(sqd[oc][0], dd[oc][0], dd[oc][0], ALU.mult), reads=[dd[oc][1]], writes=[sqd[oc][1]])

                        def tail(h=h, qt=qt):
                            RSB = 6
                            for oc in range(2):
                                P.op(PE, MM(banks[RSB][:, :], ONES, sqd[oc][0], start=(oc == 0), stop=(oc == 1)),
                                     reads=[sqd[oc][1], t_cb], writes=[tbank[RSB]])
                            P.op(ACT, AC(rs_ap, banks[RSB][:, :], AF.Ln, bias=eps5, scale=1.0 / 256.0),
                                 reads=[tbank[RSB], t_eps], writes=[t_rs])
                            P.op(ACT, AC(rs_ap, rs_ap, AF.Exp, scale=-0.5), reads=[t_rs], writes=[t_rs])
                            for oc in range(2):
                                st, t_st = as_r.next()
                                P.op(DVE, STT(st, dd[oc][0], gsub[:, oc:oc + 1], rs_ap, ALU.mult, ALU.mult),
                                     reads=[dd[oc][1], t_gs, t_rs], writes=[t_st])
                                ch = 2 * h + oc
                                P.dma(SP, DM(at_s[ch][:, qt * TT:(qt + 1) * TT], st), t_st, reads=[t_st], writes=[t_at_s[ch]])
                        tail_pending.append(tail)

        load_head(0)
        for ji, (kind, h) in enumerate(jobs):
            if ji + 1 < len(jobs):
                load_head(ji + 1)
            attend(kind, h, *hbufs[ji % 2])
        while tail_pending:
            tail_pending.pop(0)()

        if stop == 2:
            P.fence()
            P.emit(nc)
            return nc
        P.fence()
        o = 0
        act0, o = carve(o, ND * TT)
        slb0, o = carve(o, 4 * SLABE)
        ut0, o = carve(o, 2 * KS * TT)
        cq0, o = carve(o, NCH * TT)
        co0, o = carve(o, NCH * TT)
        ck0, o = carve(o, NCH * NMEM)
        cv0, o = carve(o, 2 * 512)
        pb0, o = carve(o, 3 * TT)
        sqb0, o = carve(o, 2 * TT)
        assert o <= NB_ARENA, (o, NB_ARENA)
        actT = AB[:, act0:act0 + ND * TT].rearrange("p (c t) -> p c t", c=ND)
        t_act = [T("act%d" % c) for c in range(ND)]
        slabsB = [(AB[:, slb0 + i * SLABE: slb0 + (i + 1) * SLABE], T("slabB%d" % i)) for i in range(4)]
        uT = [(AB[:, ut0 + i * KS * TT: ut0 + (i + 1) * KS * TT].rearrange("p (c t) -> p c t", c=KS),
               [T("u%d_%d" % (i, c)) for c in range(KS)]) for i in range(2)]
        cqT = AB[:, cq0:cq0 + NCH * TT].rearrange("p (c t) -> p c t", c=NCH)
        t_cq = [T("cq%d" % c) for c in range(NCH)]
        coT = AB[:, co0:co0 + NCH * TT].rearrange("p (c t) -> p c t", c=NCH)
        t_co = [T("co%d" % c) for c in range(NCH)]
        ckT = AB[:, ck0:ck0 + NCH * NMEM].rearrange("p (c t) -> p c t", c=NCH)
        t_ck = [T("ck%d" % c) for c in range(NCH)]
        cv = AB[:, cv0:cv0 + 2 * 512].rearrange("p (b d) -> p b d", b=2)
        t_cv = [T("cv%d" % c) for c in range(2)]
        pb_r = Ring([(AB[:, pb0 + i * TT: pb0 + (i + 1) * TT], T("pb%d" % i)) for i in range(3)])
        sq_rB = Ring([(AB[:, sqb0 + i * TT: sqb0 + (i + 1) * TT], T("sqb%d" % i)) for i in range(2)])
        f = 128
        x0, f = carve(f, ND * TT)
        rtb0, f = carve(f, TT)
        rlb0, f = carve(f, TT)
        tmp0, f = carve(f, 2 * TT)
        xsb0, f = carve(f, 2 * TT)
        assert f <= NF_ARENA, (f, NF_ARENA)
        xTt = AFp[:, x0:x0 + ND * TT].rearrange("p (c t) -> p c t", c=ND)
        t_x = [T("x%d" % c) for c in range(ND)]
        rtB, t_rtB = AFp[:, rtb0:rtb0 + TT], T("rtB")
        rlB, t_rlB = AFp[:, rlb0:rlb0 + TT], T("rlB")
        tmp_r = Ring([(AFp[:, tmp0 + i * TT: tmp0 + (i + 1) * TT], T("tmp%d" % i)) for i in range(2)])
        xs_rB = Ring([(AFp[:, xsb0 + i * TT: xsb0 + (i + 1) * TT], T("xsb%d" % i)) for i in range(2)])
        g_r = Ring([0, 1, 2, 3])
        OBk, LBk, RBk = 4, 5, 6

        mnT = actT
        t_mn = [[t_act[c]] for c in range(ND)]
        build_hT(memT, mnT, t_mn, G_MEM, 1, NMEM, xs_rB, sq_rB, rtB, t_rtB, RBk)

        def ck_body(i, slab, t_slab):
            b = g_r.next()
            for c in range(ND):
                P.op(PE, MM(banks[b][:, 0:NMEM], slab[:, c * 128:(c + 1) * 128], mnT[:, c, 0:NMEM],
                            start=(c == 0), stop=(c == ND - 1)), reads=[t_slab, t_act[c]], writes=[tbank[b]])
            P.op(ACT, AC(ckT[:, i, :], banks[b][:, 0:NMEM], AF.Copy), reads=[tbank[b]], writes=[t_ck[i]])
        run_slabs(slabsB, [(w_ck_r[i], ND * 128) for i in range(NCH)], ck_body)

        cvb = [g_r.next(), g_r.next()]

        def cv_body(i, slab, t_slab):
            sl = slab[:, 0:CVK * 512].rearrange("p (k n) -> p k n", k=CVK)
            for mb in range(2):
                for k in range(CVK):
                    c = i * CVK + k
                    P.op(PE, MM(banks[cvb[mb]][:, :], mnT[:, c, mb * 128:(mb + 1) * 128], sl[:, k, :],
                                start=(c == 0), stop=(c == ND - 1)), reads=[t_slab, t_act[c]], writes=[tbank[cvb[mb]]])
        run_slabs(slabsB, [(w_cv_r[i], CVK * 512) for i in range(NCV)], cv_body)
        for mb in range(2):
            P.op(ACT, AC(cv[:, mb, :], banks[cvb[mb]][:, :], AF.Copy), reads=[tbank[cvb[mb]]], writes=[t_cv[mb]])

        def stats_x():
            for c in range(ND):
                sq, t_sq = sq_rB.next()
                P.op(ACT, AC(sq, xTt[:, c, :], AF.Square), reads=[t_x[c]], writes=[t_sq])
                P.op(PE, MM(banks[RBk][:, :], ONES, sq, start=(c == 0), stop=(c == ND - 1)),
                     reads=[t_sq, t_cb], writes=[tbank[RBk]])
            rstd_from_bank(banks[RBk][:, :], tbank[RBk], rtB, t_rtB, 1.0 / D, eps6)

        def norm_to_act(g_off):
            stats_x()
            for c in range(ND):
                P.op(DVE, STT(actT[:, c, :], xTt[:, c, :], GV[:, g_off + c:g_off + c + 1], rtB, ALU.mult, ALU.mult),
                     reads=[t_x[c], t_rtB, t_gv], writes=[t_act[c]])

        def acc_x(b, oc):
            P.op(DVE, TTO(xTt[:, oc, :], banks[b][:, :], xTt[:, oc, :], ALU.add),
                 reads=[tbank[b], t_x[oc]], writes=[t_x[oc]])

        GX = max(1, ND // 4)
        t_xg = [T("xg%d" % g) for g in range(ND // GX)]
        t_ag = [T("ag%d" % g) for g in range(ND // GX)]
        def load_x(tt, g):
            c0, c1 = g * GX, (g + 1) * GX
            P.dma(SP, DM(xTt[:, c0:c1, :], xT_own[c0:c1, :, tt * TT:(tt + 1) * TT].rearrange("c p t -> p c t")),
                  t_xg[g], writes=t_x[c0:c1])

        def load_at(tt, g):
            c0, c1 = g * GX, (g + 1) * GX
            P.dma(SP, DM(actT[:, c0:c1, :], at_s[c0:c1, :, tt * TT:(tt + 1) * TT].rearrange("c p t -> p c t")),
                  t_ag[g], reads=t_at_s[c0:c1], writes=t_act[c0:c1])

        for tt in range(2):
            if tt == 0:
                for g in range(ND // GX):
                    load_at(tt, g)
                for g in range(ND // GX):
                    load_x(tt, g)

            def wout_body(oc, slab, t_slab):
                b = g_r.next()
                for kc in range(ND):
                    P.op(PE, MM(banks[b][:, :], slab[:, kc * 128:(kc + 1) * 128], actT[:, kc, :],
                                start=(kc == 0), stop=(kc == ND - 1)), reads=[t_slab, t_act[kc]], writes=[tbank[b]])
                acc_x(b, oc)
            run_slabs(slabsB, [(w_out_r[oc], ND * 128) for oc in range(ND)], wout_body)

            norm_to_act(G_CROSS)

            def cq_body(i, slab, t_slab):
                b = g_r.next()
                for c in range(ND):
                    P.op(PE, MM(banks[b][:, :], slab[:, c * 128:(c + 1) * 128], actT[:, c, :],
                                start=(c == 0), stop=(c == ND - 1)), reads=[t_slab, t_act[c]], writes=[tbank[b]])
                P.op(ACT, AC(cqT[:, i, :], banks[b][:, :], AF.Copy), reads=[tbank[b]], writes=[t_cq[i]])
            run_slabs(slabsB, [(w_cq_r[i], ND * 128) for i in range(NCH)], cq_body)

            for hh in range(NCH):
                for mb in range(2):
                    sb = g_r.next()
                    P.op(PE, MM(banks[sb][:, :], ckT[:, hh, mb * 128:(mb + 1) * 128], cqT[:, hh, :]),
                         reads=[t_ck[hh], t_cq[hh]], writes=[tbank[sb]])
                    pt, t_pt = pb_r.next()
                    P.op(ACT, AC(pt, banks[sb][:, :], AF.Exp, scale=scale), reads=[tbank[sb]], writes=[t_pt])
                    P.op(PE, MM(banks[OBk][:, :], cv[:, mb, hh * 128:(hh + 1) * 128], pt, start=(mb == 0), stop=(mb == 1)),
                         reads=[t_cv[mb], t_pt], writes=[tbank[OBk]])
                    P.op(PE, MM(banks[LBk][:, :], ONES, pt, start=(mb == 0), stop=(mb == 1)),
                         reads=[t_pt, t_cb], writes=[tbank[LBk]])
                P.op(DVE, RC(rlB, banks[LBk][:, :]), reads=[tbank[LBk]], writes=[t_rlB])
                P.op(DVE, TTO(coT[:, hh, :], banks[OBk][:, :], rlB, ALU.mult), reads=[tbank[OBk], t_rlB], writes=[t_co[hh]])

            def wco_body(g, slab, t_slab):
                sl = slab[:, 0:NCH * GC].rearrange("p (k n) -> p k n", k=NCH)
                for j in range(GC // 128):
                    oc = g * (GC // 128) + j
                    b = g_r.next()
                    for kc in range(NCH):
                        P.op(PE, MM(banks[b][:, :], sl[:, kc, j * 128:(j + 1) * 128], coT[:, kc, :],
                                    start=(kc == 0), stop=(kc == NCH - 1)), reads=[t_slab, t_co[kc]], writes=[tbank[b]])
                    acc_x(b, oc)
            run_slabs(slabsB, [(w_co_r[g], NCH * GC) for g in range(NGC)], wco_body)

            norm_to_act(G_MLP)
            srcs = []
            order = []
            for s in range(NSLAB + 1):
                if s < NSLAB:
                    for hcl in range(KS):
                        order.append(("up", s, hcl))
                        srcs.append((w_up_r[s * KS + hcl], ND * 128))
                if s >= 1:
                    for og in range(NOG):
                        order.append(("dn", s - 1, og))
                        srcs.append((w_dn_r[(s - 1) * NOG + og], KS * OG))

            def mlp_body(i, slab, t_slab):
                kind, s, j = order[i]
                u_ap, t_u = uT[s % 2]
                if kind == "up":
                    b = g_r.next()
                    for c in range(ND):
                        P.op(PE, MM(banks[b][:, :], slab[:, c * 128:(c + 1) * 128], actT[:, c, :],
                                    start=(c == 0), stop=(c == ND - 1)), reads=[t_slab, t_act[c]], writes=[tbank[b]])
                    tm, t_tm = tmp_r.next()
                    P.op(ACT, AC(tm, banks[b][:, :], AF.Relu), reads=[tbank[b]], writes=[t_tm])
                    P.op(DVE, TTO(u_ap[:, j, :], tm, tm, ALU.mult), reads=[t_tm], writes=[t_u[j]])
                else:
                    sl = slab[:, 0:KS * OG].rearrange("p (k n) -> p k n", k=KS)
                    for jj in range(OG // 128):
                        oc = j * (OG // 128) + jj
                        b = g_r.next()
                        for kc in range(KS):
                            P.op(PE, MM(banks[b][:, :], sl[:, kc, jj * 128:(jj + 1) * 128], u_ap[:, kc, :],
                                        start=(kc == 0), stop=(kc == KS - 1)), reads=[t_slab, t_u[kc]], writes=[tbank[b]])
                        acc_x(b, oc)
            run_slabs(slabsB, srcs, mlp_body)

            if tt + 1 < 2:
                for g in range(ND // GX):
                    load_at(tt + 1, g)
            stats_x()
            for c in range(ND):
                ys, t_ys = tmp_r.next()
                P.op(DVE, STT(ys, xTt[:, c, :], GV[:, G_FIN + c:G_FIN + c + 1], rtB, ALU.mult, ALU.mult),
                     reads=[t_x[c], t_rtB, t_gv], writes=[t_ys])
                P.dma(SP, DM(yT[c][:, tt * TT:(tt + 1) * TT], ys), t_ys, reads=[t_ys], writes=[T("y")])
                if tt + 1 < 2 and (c + 1) % GX == 0:
                    load_x(tt + 1, c // GX)
        P.fence()
        P.emit(nc)
    return nc


def _slabs_cols(w, ND, ncols_chunk=128):
    K, N = w.shape
    a = w.reshape(K // 128, 128, N // 128, 128)
    return np.ascontiguousarray(a.transpose(2, 1, 0, 3)).reshape(N // 128, 128, (K // 128) * 128)


def _slabs_rows(w, kper, ncol):
    K, N = w.shape
    ns = K // 128 // kper
    ng = N // ncol
    a = w.reshape(ns, kper, 128, ng, ncol)
    return np.ascontiguousarray(a.transpose(0, 3, 2, 1, 4)).reshape(ns * ng, 128, kper * ncol)


def _consts():
    ki = np.arange(128)[:, None]
    j = np.arange(19 * 128)[None, :]
    d = j - 384 - ki
    c = ((d >= 0) & (d <= 128)).astype(np.float32) + ((d >= 0) & (d <= 512) & (d % 4 == 0)) + \
        ((d >= 0) & (d <= 2048) & (d % 16 == 0))
    q = np.arange(128)[None, :]
    tri = (q >= ki).astype(np.float32)
    perm = np.zeros((128, 128), np.float32)
    perm[(np.arange(128) + 64) % 128, np.arange(128)] = 1.0
    ones = np.ones((128, 128), np.float32)
    ident = np.eye(128, dtype=np.float32)
    return np.concatenate([c, tri, perm, ones, ident], axis=1).astype(ml_dtypes.bfloat16)


def _rope_tables(pos):
    inv_freq = (10000.0 ** (-np.arange(0, HD, 2, dtype=np.float32) / HD)).astype(np.float32)
    ang = pos.astype(np.float32)[:, None] * inv_freq[None, :]
    cos = np.cos(ang).astype(np.float32)
    sin = np.sin(ang).astype(np.float32)
    cosT = np.concatenate([cos, cos], axis=1).T
    sinT = np.concatenate([-sin, sin], axis=1).T
    return np.ascontiguousarray(cosT), np.ascontiguousarray(sinT)


_PROG_CACHE = {}
_STOP = 99


def _run(inp, D, B):
    ND = D // 128
    f32 = np.float32
    x = np.asarray(inp["x"], f32)
    mem = np.asarray(inp["mem"], f32)
    KS = min(8, ND)
    OG = min(512, D)
    GC = min(1024, D)
    CVK = min(8, ND)
    w_in = np.asarray(inp["w_in"], f32)[0]
    w_ckv = np.asarray(inp["w_ckv"], f32)[0]
    shared = {
        "w_in_r": _slabs_cols(w_in, ND),
        "w_out_r": _slabs_cols(np.asarray(inp["w_out"], f32)[0], ND),
        "w_cq_r": _slabs_cols(np.asarray(inp["w_cq"], f32)[0], ND),
        "w_ck_r": _slabs_cols(np.ascontiguousarray(w_ckv[:, :512]), ND),
        "w_cv_r": _slabs_rows(np.ascontiguousarray(w_ckv[:, 512:]), CVK, 512),
        "w_co_r": _slabs_rows(np.asarray(inp["w_co"], f32)[0], NCH, GC),
        "w_up_r": _slabs_cols(np.asarray(inp["w_up"], f32)[0], ND),
        "w_dn_r": _slabs_rows(np.asarray(inp["w_down"], f32)[0], KS, OG),
        "cb": _consts(),
    }
    gv = np.concatenate([
        np.asarray(inp["norm_mix"], f32)[0].reshape(ND, 128).T,
        np.asarray(inp["norm_cross"], f32)[0].reshape(ND, 128).T,
        np.asarray(inp["norm_mem"], f32)[0].reshape(ND, 128).T,
        np.asarray(inp["norm_mlp"], f32)[0].reshape(ND, 128).T,
        np.asarray(inp["norm_final"], f32).reshape(ND, 128).T,
        np.asarray(inp["diff_subln"], f32)[0].reshape(2, 128).T,
    ], axis=1)
    shared["gv"] = np.ascontiguousarray(gv)
    shared["lp"] = np.ascontiguousarray(np.broadcast_to(np.asarray(inp["diff_lambda"], f32)[0].reshape(1, 512), (128, 512)))
    in_maps = []
    for core in range(2 * B):
        b, qh = core // 2, core % 2
        own = slice(qh * HALF, (qh + 1) * HALF)
        oth = slice((1 - qh) * HALF, (2 - qh) * HALF)
        m = dict(shared)
        m["xT_own"] = np.ascontiguousarray(x[b, own].T).reshape(ND, 128, HALF)
        m["xT_oth"] = np.ascontiguousarray(x[b, oth].T).reshape(ND, 128, HALF)
        m["memT"] = np.ascontiguousarray(mem[b].T).reshape(ND, 128, NMEM)
        m["cos_own"], m["sin_own"] = _rope_tables(np.arange(own.start, own.stop))
        m["cos_oth"], m["sin_oth"] = _rope_tables(np.arange(oth.start, oth.stop))
        m["visb"] = np.full((128, 1), 0.0 if qh == 1 else NEG, f32)
        in_maps.append(m)
    if D not in _PROG_CACHE:
        _PROG_CACHE[D] = build_program(D, _STOP)
    nc = _PROG_CACHE[D]
    res = run_bass_kernel_spmd(nc, in_maps, core_ids=list(range(2 * B)))
    out = np.empty((B, S, D), f32)
    for core in range(2 * B):
        b, qh = core // 2, core % 2
        out[b, qh * HALF:(qh + 1) * HALF] = np.asarray(res.results[core]["yT"]).reshape(D, HALF).T
    return out


def kernel(**inputs):
    return _run(inputs, 4096, 4)
```

```python
import math
from contextlib import ExitStack

import numpy as np
import ml_dtypes
import concourse.bass as bass
import concourse.mybir as mybir
from concourse.bass_utils import run_bass_kernel_spmd

F32 = mybir.dt.float32
BF16 = mybir.dt.bfloat16
AF = mybir.ActivationFunctionType
ALU = mybir.AluOpType
PE, ACT, DVE, POOL, SP = "pe", "act", "dve", "pool", "sp"

S = 2048
HALF = 1024
TT = 512
HD = 128
NMEM = 256
NCH = 4
LAMBDA_INIT = 0.8 - 0.6 * math.exp(-0.3 * 0)
NEG = -30000.0


class T:
    __slots__ = ("name", "writer", "readers", "sem", "dma_cnt")

    def __init__(self, name):
        self.name = name
        self.writer = None
        self.readers = []
        self.sem = None
        self.dma_cnt = 0


class Op:
    __slots__ = ("eng", "fn", "deps", "is_dma", "dst", "needs_inc", "tok")

    def __init__(self, eng, fn, is_dma=False, dst=None):
        self.eng = eng
        self.fn = fn
        self.deps = []
        self.is_dma = is_dma
        self.dst = dst
        self.needs_inc = False
        self.tok = None


class Prog:
    def __init__(self):
        self.ops = {PE: [], ACT: [], DVE: [], POOL: [], SP: []}
        self.dma_tiles = []
        self.last_dma = {}
        self.fence_ap = None

    def _link(self, op, deps):
        seen = set()
        for d in deps:
            if d is op or id(d) in seen:
                continue
            seen.add(id(d))
            if d.eng == PE and op.eng == PE and not d.is_dma and not op.is_dma:
                continue
            op.deps.append(d)
            if not d.is_dma:
                d.needs_inc = True

    def _add(self, op, reads, writes):
        deps = []
        for t in reads:
            if t.writer is not None:
                deps.append(t.writer)
        for t in writes:
            if t.writer is not None:
                deps.append(t.writer)
            deps.extend(t.readers)
        self._link(op, deps)
        for t in writes:
            t.writer = op
            t.readers = []
        for t in reads:
            if t.writer is not op:
                t.readers.append(op)
        self.ops[op.eng].append(op)
        return op

    def op(self, eng, fn, reads=(), writes=()):
        return self._add(Op(eng, fn), list(reads), list(writes))

    def dma(self, eng, fn, dst, reads=(), writes=()):
        if dst.sem is None:
            dst.sem = -1
            self.dma_tiles.append(dst)
        o = Op(eng, fn, is_dma=True, dst=dst)
        self._add(o, list(reads), list(writes))
        self.last_dma[id(dst)] = o
        return o

    def fence(self):
        deps = []
        for e in (PE, ACT, DVE, POOL):
            for o in reversed(self.ops[e]):
                if not o.is_dma and o.fn is not None:
                    deps.append(o)
                    break
        deps.extend(self.last_dma.values())
        self.last_dma = {}
        ap = self.fence_ap
        j = Op(POOL, lambda e: e.memset(ap, 0.0))
        self._link(j, deps)
        self.ops[POOL].append(j)
        j.needs_inc = True
        for e in (PE, ACT, DVE, SP):
            w = Op(e, None)
            w.deps.append(j)
            self.ops[e].append(w)

    def emit(self, nc):
        with ExitStack() as es:
            esem = {}
            for e in (PE, ACT, DVE, POOL):
                esem[e] = es.enter_context(nc.semaphore("s_" + e))
            for i, t in enumerate(self.dma_tiles):
                t.sem = es.enter_context(nc.semaphore("d%d" % i))
            for e, lst in self.ops.items():
                cnt = 0
                for o in lst:
                    if o.is_dma:
                        o.dst.dma_cnt += 1
                        o.tok = (o.dst.sem, 16 * o.dst.dma_cnt)
                    elif o.needs_inc:
                        cnt += 1
                        o.tok = (esem[e], cnt)
            block = es.enter_context(nc.Block())
            handles = {PE: block.tensor, ACT: block.scalar, DVE: block.vector,
                       POOL: block.gpsimd, SP: block.sync}
            for e, lst in self.ops.items():
                if not lst:
                    continue

                def body(eng, lst=lst):
                    known = {}
                    for o in lst:
                        for d in o.deps:
                            sem, val = d.tok
                            k = id(sem)
                            if known.get(k, 0) >= val:
                                continue
                            known[k] = val
                            eng.wait_ge(sem, val)
                        if o.fn is None:
                            continue
                        ins = o.fn(eng)
                        if o.tok is not None:
                            ins.then_inc(o.tok[0], 16 if o.is_dma else 1)
                handles[e](body)


class Ring:
    def __init__(self, items):
        self.items = items
        self.i = 0

    def next(self):
        it = self.items[self.i % len(self.items)]
        self.i += 1
        return it


def MM(out, lhsT, rhs, start=True, stop=True):
    return lambda e: e.matmul(out, lhsT=lhsT, rhs=rhs, start=start, stop=stop)


def TR(out, in_, ident):
    return lambda e: e.transpose(out, in_, ident)


def AC(out, in_, func, bias=None, scale=None):
    kw = {}
    if bias is not None:
        kw["bias"] = bias
    if scale is not None:
        kw["scale"] = scale
    return lambda e: e.activation(out=out, in_=in_, func=func, **kw)


def TTO(out, in0, in1, op):
    return lambda e: e.tensor_tensor(out=out, in0=in0, in1=in1, op=op)


def STT(out, in0, scalar, in1, op0, op1):
    return lambda e: e.scalar_tensor_tensor(out=out, in0=in0, scalar=scalar, in1=in1, op0=op0, op1=op1)


def TS(out, in0, s1, op0, op1=None, accum_out=None):
    kw = {}
    if op1 is not None:
        kw["op1"] = op1
    if accum_out is not None:
        kw["accum_out"] = accum_out
    return lambda e: e.tensor_scalar(out=out, in0=in0, scalar1=s1, scalar2=None, op0=op0, **kw)


def RC(out, in_):
    return lambda e: e.reciprocal(out=out, in_=in_)


def CP(out, in_):
    return lambda e: e.tensor_copy(out=out, in_=in_)


def MS(ap, v):
    return lambda e: e.memset(ap, v)


def DM(out, in_, cast=False):
    if cast:
        return lambda e: e.dma_start(out=out, in_=in_, max_dma_last_dim=8192)
    return lambda e: e.dma_start(out=out, in_=in_)


def build_program(D, stop=99):
    ND = D // 128
    NQ = D // 256
    HDIFF = D // 512
    HDIL = D // 256
    DFF = 4 * D
    NFF = DFF // 128
    KS = min(8, ND)
    NSLAB = NFF // KS
    OG = min(512, D)
    NOG = D // OG
    GC = min(1024, D)
    NGC = D // GC
    CVK = min(8, ND)
    NCV = ND // CVK
    SLABE = 4096
    assert ND * 128 <= SLABE and KS * OG <= SLABE and NCH * GC <= SLABE and CVK * 512 <= SLABE

    nc = bass.Bass("TRN2", target_bir_lowering=False)

    def din(name, shape, dt=F32):
        return nc.dram_tensor(name, list(shape), dt, kind="ExternalInput").ap()

    xT_own = din("xT_own", [ND, 128, HALF])
    xT_oth = din("xT_oth", [ND, 128, HALF])
    memT = din("memT", [ND, 128, NMEM])
    cos_own = din("cos_own", [128, HALF])
    sin_own = din("sin_own", [128, HALF])
    cos_oth = din("cos_oth", [128, HALF])
    sin_oth = din("sin_oth", [128, HALF])
    visb_d = din("visb", [128, 1])
    cb_d = din("cb", [128, 23 * 128], BF16)
    gv_d = din("gv", [128, 5 * ND + 2])
    lp_d = din("lp", [128, 4 * 128])
    w_in_r = din("w_in_r", [3 * D // 128, 128, ND * 128])
    w_out_r = din("w_out_r", [ND, 128, ND * 128])
    w_cq_r = din("w_cq_r", [NCH, 128, ND * 128])
    w_ck_r = din("w_ck_r", [NCH, 128, ND * 128])
    w_cv_r = din("w_cv_r", [NCV, 128, CVK * 512])
    w_co_r = din("w_co_r", [NGC, 128, NCH * GC])
    w_up_r = din("w_up_r", [NFF, 128, ND * 128])
    w_dn_r = din("w_dn_r", [NSLAB * NOG, 128, KS * OG])
    yT = nc.dram_tensor("yT", [ND, 128, HALF], F32, kind="ExternalOutput").ap()
    qT_s = nc.dram_tensor("qT_s", [2 * NQ, 128, HALF], BF16).ap()
    kT_s = nc.dram_tensor("kT_s", [2 * NQ, 128, S], BF16).ap()
    v_s = nc.dram_tensor("v_s", [2 * NQ, 128, 16 * 128], BF16).ap()
    at_s = nc.dram_tensor("at_s", [ND, 128, HALF], BF16).ap()

    NB_ARENA = max(ND * HALF + 4 * SLABE + 7 * TT, ND * TT + 4 * SLABE + 2 * KS * TT + 8 * TT + 2048 + 5 * TT, 26624)
    NF_ARENA = max(128 + ND * TT + 6 * TT, 128 + 2 * HALF + 9 * TT)

    with ExitStack() as es:
        AB = es.enter_context(nc.sbuf_tensor("arena_b", [128, NB_ARENA], BF16))
        AFp = es.enter_context(nc.sbuf_tensor("arena_f", [128, NF_ARENA], F32))
        CB = es.enter_context(nc.sbuf_tensor("cb_sb", [128, 23 * 128], BF16))
        GV = es.enter_context(nc.sbuf_tensor("gv_sb", [128, 5 * ND + 2], F32))
        SM = es.enter_context(nc.sbuf_tensor("small", [128, 16], F32))
        LP = es.enter_context(nc.sbuf_tensor("lp_sb", [128, 4 * 128], F32))
        banks = [es.enter_context(nc.psum_tensor("bank%d" % i, [128, 512], F32)) for i in range(7)]
        bankT = es.enter_context(nc.psum_tensor("bankT", [128, 1024], BF16))
        banks.append(None)
        tbank = [T("bank%d" % i) for i in range(8)]

        P = Prog()
        P.fence_ap = SM[:, 15:16]

        Mmask = CB[:, 0:19 * 128]
        TRI = CB[:, 19 * 128:20 * 128]
        PERM = CB[:, 20 * 128:21 * 128]
        ONES = CB[:, 21 * 128:22 * 128]
        IDENT = CB[:, 22 * 128:23 * 128]
        t_cb, t_gv, t_lp = T("cb"), T("gv"), T("lp")
        G_MIX, G_CROSS, G_MEM, G_MLP, G_FIN, G_SUB = 0, ND, 2 * ND, 3 * ND, 4 * ND, 5 * ND
        visb = SM[:, 0:1]
        neglam = SM[:, 1:2]
        eps6 = SM[:, 7:8]
        eps5 = SM[:, 8:9]
        gsub = SM[:, 9:11]

        P.dma(SP, DM(CB[:], cb_d), t_cb, writes=[t_cb])
        P.dma(SP, DM(GV[:], gv_d), t_gv, writes=[t_gv])
        P.dma(SP, DM(LP[:], lp_d), t_lp, writes=[t_lp])
        t_vis = T("vis")
        P.dma(SP, DM(visb, visb_d), t_vis, writes=[t_vis])
        t_s1, t_s2, t_e, t_d, t_nl, t_eps, t_gs = T("s1"), T("s2"), T("e"), T("d"), T("nl"), T("eps"), T("gs")
        junk = AFp[:, 0:128]
        t_junk = T("junk")
        P.op(DVE, TTO(junk, LP[:, 0:128], LP[:, 128:256], ALU.mult), reads=[t_lp], writes=[t_junk])
        P.op(DVE, TS(junk, junk, 1.0, ALU.mult, op1=ALU.add, accum_out=SM[:, 2:3]), reads=[t_junk], writes=[t_junk, t_s1])
        P.op(DVE, TTO(junk, LP[:, 256:384], LP[:, 384:512], ALU.mult), reads=[t_lp], writes=[t_junk])
        P.op(DVE, TS(junk, junk, 1.0, ALU.mult, op1=ALU.add, accum_out=SM[:, 3:4]), reads=[t_junk], writes=[t_junk, t_s2])
        P.op(ACT, AC(SM[:, 4:6], SM[:, 2:4], AF.Exp), reads=[t_s1, t_s2], writes=[t_e])
        P.op(DVE, TTO(SM[:, 6:7], SM[:, 5:6], SM[:, 4:5], ALU.subtract), reads=[t_e], writes=[t_d])
        P.op(DVE, TS(neglam, SM[:, 6:7], -LAMBDA_INIT, ALU.add), reads=[t_d], writes=[t_nl])
        P.op(DVE, MS(eps6, 1e-6), writes=[t_eps])
        P.op(DVE, MS(eps5, 1e-5), writes=[t_eps])
        P.op(DVE, TS(gsub, GV[:, G_SUB:G_SUB + 2], 1.0 - LAMBDA_INIT, ALU.mult), reads=[t_gv], writes=[t_gs])

        def carve(base, n):
            return base, base + n

        def rstd_from_bank(bk_ap, tb_, rt, t_rt_, inv_n, eps_ap):
            P.op(ACT, AC(rt, bk_ap, AF.Sqrt, bias=eps_ap, scale=inv_n), reads=[tb_, t_eps], writes=[t_rt_])
            P.op(DVE, RC(rt, rt), reads=[t_rt_], writes=[t_rt_])

        def run_slabs(slots, srcs, body):
            look = len(slots) - 1
            q = []
            nxt = 0
            for i in range(len(srcs)):
                while nxt < len(srcs) and nxt <= i + look:
                    ap, t = slots[run_slabs.n % len(slots)]
                    run_slabs.n += 1
                    src_ap, nel = srcs[nxt]
                    P.dma(POOL, DM(ap[:, 0:nel], src_ap, cast=True), t, writes=[t])
                    q.append((ap, t))
                    nxt += 1
                ap, t = q.pop(0)
                body(i, ap, t)
        run_slabs.n = 0

        def build_hT(xsrc, dst, t_dst, g_off, ntile, width, xs_ring, sq_ring, rt, t_rt_, rbank):
            for t in range(ntile):
                for c in range(ND):
                    xs, t_xs = xs_ring.next()
                    sq, t_sq = sq_ring.next()
                    P.dma(SP, DM(xs[:, 0:width], xsrc[c][:, t * width:(t + 1) * width]), t_xs, writes=[t_xs])
                    P.op(ACT, AC(sq[:, 0:width], xs[:, 0:width], AF.Square), reads=[t_xs], writes=[t_sq])
                    P.op(PE, MM(banks[rbank][:, 0:width], ONES, sq[:, 0:width], start=(c == 0), stop=(c == ND - 1)),
                         reads=[t_sq, t_cb], writes=[tbank[rbank]])
                rstd_from_bank(banks[rbank][:, 0:width], tbank[rbank], rt[:, 0:width], t_rt_, 1.0 / D, eps6)
                for c in range(ND):
                    xs, t_xs = xs_ring.next()
                    P.dma(SP, DM(xs[:, 0:width], xsrc[c][:, t * width:(t + 1) * width]), t_xs, writes=[t_xs])
                    P.op(DVE, STT(dst[:, c, t * width:(t + 1) * width], xs[:, 0:width], GV[:, g_off + c:g_off + c + 1],
                                  rt[:, 0:width], ALU.mult, ALU.mult),
                         reads=[t_xs, t_rt_, t_gv], writes=[t_dst[c][t]])

        if stop == 0:
            P.fence()
            P.emit(nc)
            return nc
        o = 0
        hT0, o = carve(o, ND * HALF)
        slab0, o = carve(o, 4 * SLABE)
        sq0, o = carve(o, 2 * TT)
        tb0, o = carve(o, 2 * TT)
        st0, o = carve(o, 3 * TT)
        assert o <= NB_ARENA, (o, NB_ARENA)
        hT = AB[:, hT0:hT0 + ND * HALF].rearrange("p (c t) -> p c t", c=ND)
        t_h = [[T("h%d_%d" % (c, t)) for t in range(2)] for c in range(ND)]
        slabsA = [(AB[:, slab0 + i * SLABE: slab0 + (i + 1) * SLABE], T("slab%d" % i)) for i in range(4)]
        sq_rA = Ring([(AB[:, sq0 + i * TT: sq0 + (i + 1) * TT], T("sq%d" % i)) for i in range(2)])
        tb_r = Ring([(AB[:, tb0 + i * TT: tb0 + (i + 1) * TT], T("tb%d" % i)) for i in range(2)])
        st_r = Ring([(AB[:, st0 + i * TT: st0 + (i + 1) * TT], T("st%d" % i)) for i in range(3)])
        f = 128
        cosT, f = carve(f, HALF)
        sinT, f = carve(f, HALF)
        xs0, f = carve(f, 4 * TT)
        rt0, f = carve(f, TT)
        t10, f = carve(f, 2 * TT)
        t20, f = carve(f, 2 * TT)
        assert f <= NF_ARENA, (f, NF_ARENA)
        cos_sb = AFp[:, cosT:cosT + HALF]
        sin_sb = AFp[:, sinT:sinT + HALF]
        t_cos, t_sin = T("cos"), T("sin")
        xs_rA = Ring([(AFp[:, xs0 + i * TT: xs0 + (i + 1) * TT], T("xs%d" % i)) for i in range(4)])
        rtA, t_rtA = AFp[:, rt0:rt0 + TT], T("rt")
        t1_r = Ring([(AFp[:, t10 + i * TT: t10 + (i + 1) * TT], T("t1_%d" % i)) for i in range(2)])
        t2_r = Ring([(AFp[:, t20 + i * TT: t20 + (i + 1) * TT], T("t2_%d" % i)) for i in range(2)])
        proj_r = Ring([0, 1, 2, 6])
        RB = 3
        pp_r = Ring([4, 5])
        tr_r = Ring([7])
        t_q_s = [T("q_s%d" % i) for i in range(2 * NQ)]
        t_k_s = [T("k_s%d" % i) for i in range(2 * NQ)]
        t_v_s = [T("v_s%d" % i) for i in range(2 * NQ)]
        t_at_s = [T("at_s%d" % i) for i in range(ND)]

        def chunk_kind(cc):
            r = cc // NQ
            j = cc % NQ
            return ("q", "k", "v", "q", "k", "v")[r], (j if r < 3 else NQ + j)

        for pas in (0, 1):
            own = (pas == 0)
            xsrc = xT_own if own else xT_oth
            P.dma(SP, DM(cos_sb, cos_own if own else cos_oth), t_cos, writes=[t_cos])
            P.dma(SP, DM(sin_sb, sin_own if own else sin_oth), t_sin, writes=[t_sin])
            build_hT(xsrc, hT, t_h, G_MIX, 2, TT, xs_rA, sq_rA, rtA, t_rtA, RB)
            chunks = [cc for cc in range(3 * D // 128) if own or chunk_kind(cc)[0] != "q"]
            kvoff = HALF if own else 0

            pending = []

            def flush():
                while pending:
                    pending.pop(0)()

            def body(i, slab, t_slab, chunks=chunks, own=own, kvoff=kvoff):
                cc = chunks[i]
                kind, idx = chunk_kind(cc)
                for t in range(2):
                    b = proj_r.next()
                    for c in range(ND):
                        P.op(PE, MM(banks[b][:, :], slab[:, c * 128:(c + 1) * 128], hT[:, c, t * TT:(t + 1) * TT],
                                    start=(c == 0), stop=(c == ND - 1)),
                             reads=[t_slab, t_h[c][t]], writes=[tbank[b]])
                    flush()
                    tb, t_tb = tb_r.next()
                    P.op(ACT, AC(tb, banks[b][:, :], AF.Copy), reads=[tbank[b]], writes=[t_tb])
                    pending.append(lambda b=b, tb=tb, t_tb=t_tb, t=t, kind=kind, idx=idx: post(b, tb, t_tb, t, kind, idx))

            def post(b, tb, t_tb, t, kind, idx, own=own, kvoff=kvoff):
                if True:
                    st, t_st = st_r.next()
                    if kind == "v":
                        tr = tr_r.next()
                        trb = bankT
                        for j in range(4):
                            P.op(PE, TR(trb[:, j * 128:(j + 1) * 128], tb[:, j * 128:(j + 1) * 128], IDENT),
                                 reads=[t_tb, t_cb], writes=[tbank[tr]])
                        P.op(DVE, CP(st, trb[:, 0:512]), reads=[tbank[tr]], writes=[t_st])
                        blk0 = (8 if own else 0) + t * 4
                        P.dma(SP, DM(v_s[idx][:, blk0 * 128:(blk0 + 4) * 128], st), t_st,
                              reads=[t_st], writes=[t_v_s[idx]])
                    else:
                        pp = pp_r.next()
                        P.op(PE, MM(banks[pp][:, :], PERM, tb), reads=[t_tb, t_cb], writes=[tbank[pp]])
                        t1, t_t1 = t1_r.next()
                        t2, t_t2 = t2_r.next()
                        P.op(DVE, TTO(t1, banks[b][:, :], cos_sb[:, t * TT:(t + 1) * TT], ALU.mult),
                             reads=[tbank[b], t_cos, t_tb], writes=[t_t1])
                        P.op(DVE, TTO(t2, banks[pp][:, :], sin_sb[:, t * TT:(t + 1) * TT], ALU.mult),
                             reads=[tbank[pp], t_sin], writes=[t_t2])
                        P.op(DVE, TTO(st, t1, t2, ALU.add), reads=[t_t1, t_t2], writes=[t_st])
                        if kind == "q":
                            P.dma(SP, DM(qT_s[idx][:, t * TT:(t + 1) * TT], st), t_st, reads=[t_st], writes=[t_q_s[idx]])
                        else:
                            P.dma(SP, DM(kT_s[idx][:, kvoff + t * TT: kvoff + (t + 1) * TT], st), t_st,
                                  reads=[t_st], writes=[t_k_s[idx]])

            run_slabs(slabsA, [(w_in_r[cc], ND * 128) for cc in chunks], body)
            flush()

        if stop == 1:
            P.fence()
            P.emit(nc)
            return nc
        P.fence()
        HBE = 2 * HALF + 2 * S + 2 * 16 * 128
        o = 0
        hb0, o = carve(o, 2 * HBE)
        p0, o = carve(o, 8 * TT)
        as0, o = carve(o, 2 * TT)
        sqd0, o = carve(o, 2 * TT)
        assert o <= NB_ARENA, (o, NB_ARENA)
        hbufs = []
        for i in range(2):
            base = hb0 + i * HBE
            q_ap = AB[:, base: base + 2 * HALF].rearrange("p (c t) -> p c t", c=2)
            k_ap = AB[:, base + 2 * HALF: base + 2 * HALF + 2 * S].rearrange("p (c t) -> p c t", c=2)
            v_ap = AB[:, base + 2 * HALF + 2 * S: base + HBE].rearrange("p (c b d) -> p c b d", c=2, b=16)
            hbufs.append((q_ap, k_ap, v_ap, [T("hq%d_%d" % (i, c)) for c in range(2)],
                          [T("hk%d_%d" % (i, c)) for c in range(2)], [T("hv%d_%d" % (i, c)) for c in range(2)]))
        p_r = Ring([(AB[:, p0 + i * TT: p0 + (i + 1) * TT], T("p%d" % i)) for i in range(8)])
        as_r = Ring([(AB[:, as0 + i * TT: as0 + (i + 1) * TT], T("as%d" % i)) for i in range(2)])
        sqd = [(AB[:, sqd0 + i * TT: sqd0 + (i + 1) * TT], T("sqd%d" % i)) for i in range(2)]
        f = 128
        rl0, f = carve(f, TT)
        on0, f = carve(f, 2 * TT)
        dd0, f = carve(f, 2 * TT)
        o2_0, f = carve(f, TT)
        rs0, f = carve(f, TT)
        osb0, f = carve(f, 2 * TT)
        lsb0, f = carve(f, TT)
        assert f <= NF_ARENA, (f, NF_ARENA)
        lsb, t_lsb = AFp[:, lsb0:lsb0 + TT], T("lsb")
        osb = [(AFp[:, osb0 + i * TT: osb0 + (i + 1) * TT], T("osb%d" % i)) for i in range(2)]
        rlA, t_rlA = AFp[:, rl0:rl0 + TT], T("rl")
        on = [(AFp[:, on0 + i * TT: on0 + (i + 1) * TT], T("on%d" % i)) for i in range(2)]
        dd = [(AFp[:, dd0 + i * TT: dd0 + (i + 1) * TT], T("dd%d" % i)) for i in range(2)]
        o2_ap, t_o2 = AFp[:, o2_0:o2_0 + TT], T("o2")
        rs_ap, t_rs = AFp[:, rs0:rs0 + TT], T("rs")
        OB = [0, 1]
        LB = 2
        s_r_diff = Ring([3, 4, 5])
        s_r_dil = Ring([1, 3, 4, 5])
        scale = 1.0 / math.sqrt(HD)

        jobs = [("diff", h) for h in range(HDIFF)] + [("dil", h) for h in range(HDIL)]

        def load_head(ji):
            kind, h = jobs[ji]
            q_ap, k_ap, v_ap, tq, tk, tv = hbufs[ji % 2]
            ncomp = 2 if kind == "diff" else 1
            for c in range(ncomp):
                qi = (2 * h + c) if kind == "diff" else (NQ + h)
                P.dma(SP, DM(q_ap[:, c, :], qT_s[qi]), tq[c], reads=[t_q_s[qi]], writes=[tq[c]])
                P.dma(SP, DM(k_ap[:, c, :], kT_s[qi]), tk[c], reads=[t_k_s[qi]], writes=[tk[c]])
                P.dma(SP, DM(v_ap[:, c, :, :], v_s[qi].rearrange("p (b d) -> p b d", b=16)), tv[c],
                      reads=[t_v_s[qi]], writes=[tv[c]])

        tail_pending = []
        norm_pending = []

        def attend(kind, h, q_ap, k_ap, v_ap, tq, tk, tv):
            ncomp = 2 if kind == "diff" else 1
            nvo = ncomp
            s_r = s_r_diff if kind == "diff" else s_r_dil
            LA = len(s_r.items) - 1
            for qt in range(2):
                qb0 = 8 + qt * 4
                blocks = list(range(0, qb0 + 4))
                for comp in range(ncomp):
                    sb_of = {}

                    def qk(kb):
                        c0 = max(0, kb - qb0) * 128
                        sb = s_r.next()
                        sb_of[kb] = sb
                        P.op(PE, MM(banks[sb][:, c0:TT], k_ap[:, comp, kb * 128:(kb + 1) * 128],
                                    q_ap[:, comp, qt * TT + c0:(qt + 1) * TT]),
                             reads=[tk[comp], tq[comp]], writes=[tbank[sb]])

                    for j in range(min(LA, len(blocks))):
                        qk(blocks[j])
                    for i, kb in enumerate(blocks):
                        if i == 3:
                            while norm_pending:
                                norm_pending.pop(0)()
                        if i == 8:
                            while tail_pending:
                                tail_pending.pop(0)()
                        if i + LA < len(blocks):
                            qk(blocks[i + LA])
                        c0 = max(0, kb - qb0) * 128
                        sb = sb_of[kb]
                        pt, t_pt = p_r.next()
                        if kb < 8:
                            P.op(ACT, AC(pt[:, c0:TT], banks[sb][:, c0:TT], AF.Exp, bias=visb, scale=scale),
                                 reads=[tbank[sb], t_vis], writes=[t_pt])
                        else:
                            P.op(ACT, AC(pt[:, c0:TT], banks[sb][:, c0:TT], AF.Exp, scale=scale),
                                 reads=[tbank[sb]], writes=[t_pt])
                        if kind == "dil":
                            mo = (qb0 - kb + 3) * 128 + c0
                            P.op(DVE, TTO(pt[:, c0:TT], pt[:, c0:TT], Mmask[:, mo:mo + TT - c0], ALU.mult),
                                 reads=[t_pt, t_cb], writes=[t_pt])
                        elif kb >= qb0:
                            P.op(DVE, TTO(pt[:, c0:c0 + 128], pt[:, c0:c0 + 128], TRI, ALU.mult),
                                 reads=[t_pt, t_cb], writes=[t_pt])
                        first = (i == 0)
                        last = (i == len(blocks) - 1)
                        for oc in range(nvo):
                            P.op(PE, MM(banks[OB[oc]][:, c0:TT], v_ap[:, oc, kb, :], pt[:, c0:TT], start=first, stop=last),
                                 reads=[tv[oc], t_pt], writes=[tbank[OB[oc]]])
                        P.op(PE, MM(banks[LB][:, c0:TT], ONES, pt[:, c0:TT], start=first, stop=last),
                             reads=[t_pt, t_cb], writes=[tbank[LB]])
                    P.op(DVE, CP(lsb, banks[LB][:, :]), reads=[tbank[LB]], writes=[t_lsb])
                    P.op(ACT, AC(osb[0][0], banks[OB[0]][:, :], AF.Copy), reads=[tbank[OB[0]]], writes=[osb[0][1]])
                    if nvo == 2:
                        P.op(DVE, CP(osb[1][0], banks[OB[1]][:, :]), reads=[tbank[OB[1]]], writes=[osb[1][1]])
                    def norm(kind=kind, comp=comp, h=h, qt=qt):
                        P.op(DVE, RC(rlA, lsb), reads=[t_lsb], writes=[t_rlA])
                        if kind == "dil":
                            st, t_st = as_r.next()
                            P.op(DVE, TTO(st, osb[0][0], rlA, ALU.mult), reads=[osb[0][1], t_rlA], writes=[t_st])
                            ch = D // 256 + h
                            P.dma(SP, DM(at_s[ch][:, qt * TT:(qt + 1) * TT], st), t_st, reads=[t_st], writes=[t_at_s[ch]])
                        elif comp == 0:
                            for oc in range(2):
                                P.op(DVE, TTO(on[oc][0], osb[oc][0], rlA, ALU.mult),
                                     reads=[osb[oc][1], t_rlA], writes=[on[oc][1]])
                        else:
                            for oc in range(2):
                                P.op(DVE, TTO(o2_ap, osb[oc][0], rlA, ALU.mult),
                                     reads=[osb[oc][1], t_rlA], writes=[t_o2])
                                P.op(DVE, STT(dd[oc][0], o2_ap, neglam, on[oc][0], ALU.mult, ALU.add),
                                     reads=[t_o2, t_nl, on[oc][1]], writes=[dd[oc][1]])
                                P.op(DVE, TTO(sqd[oc][0], dd[oc][0], dd[oc][0], ALU.mult), reads=[dd[oc][1]], writes=[sqd[oc][1]])

                            def tail(h=h, qt=qt):
                                RSB = 6
                                for oc in range(2):
                                    P.op(PE, MM(banks[RSB][:, :], ONES, sqd[oc][0], start=(oc == 0), stop=(oc == 1)),
                                         reads=[sqd[oc][1], t_cb], writes=[tbank[RSB]])
                                P.op(ACT, AC(rs_ap, banks[RSB][:, :], AF.Ln, bias=eps5, scale=1.0 / 256.0),
                                     reads=[tbank[RSB], t_eps], writes=[t_rs])
                                P.op(ACT, AC(rs_ap, rs_ap, AF.Exp, scale=-0.5), reads=[t_rs], writes=[t_rs])
                                for oc in range(2):
                                    st, t_st = as_r.next()
                                    P.op(DVE, STT(st, dd[oc][0], gsub[:, oc:oc + 1], rs_ap, ALU.mult, ALU.mult),
                                         reads=[dd[oc][1], t_gs, t_rs], writes=[t_st])
                                    ch = 2 * h + oc
                                    P.dma(SP, DM(at_s[ch][:, qt * TT:(qt + 1) * TT], st), t_st, reads=[t_st], writes=[t_at_s[ch]])
                            tail_pending.append(tail)
                    norm_pending.append(norm)

        load_head(0)
        for ji, (kind, h) in enumerate(jobs):
            if ji + 1 < len(jobs):
                load_head(ji + 1)
            attend(kind, h, *hbufs[ji % 2])
        while norm_pending:
            norm_pending.pop(0)()
        while tail_pending:
            tail_pending.pop(0)()

        if stop == 2:
            P.fence()
            P.emit(nc)
            return nc
        P.fence()
        o = 0
        act0, o = carve(o, ND * TT)
        slb0, o = carve(o, 4 * SLABE)
        ut0, o = carve(o, 2 * KS * TT)
        cq0, o = carve(o, NCH * TT)
        co0, o = carve(o, NCH * TT)
        ck0, o = carve(o, NCH * NMEM)
        cv0, o = carve(o, 2 * 512)
        pb0, o = carve(o, 3 * TT)
        sqb0, o = carve(o, 2 * TT)
        assert o <= NB_ARENA, (o, NB_ARENA)
        actT = AB[:, act0:act0 + ND * TT].rearrange("p (c t) -> p c t", c=ND)
        t_act = [T("act%d" % c) for c in range(ND)]
        slabsB = [(AB[:, slb0 + i * SLABE: slb0 + (i + 1) * SLABE], T("slabB%d" % i)) for i in range(4)]
        uT = [(AB[:, ut0 + i * KS * TT: ut0 + (i + 1) * KS * TT].rearrange("p (c t) -> p c t", c=KS),
               [T("u%d_%d" % (i, c)) for c in range(KS)]) for i in range(2)]
        cqT = AB[:, cq0:cq0 + NCH * TT].rearrange("p (c t) -> p c t", c=NCH)
        t_cq = [T("cq%d" % c) for c in range(NCH)]
        coT = AB[:, co0:co0 + NCH * TT].rearrange("p (c t) -> p c t", c=NCH)
        t_co = [T("co%d" % c) for c in range(NCH)]
        ckT = AB[:, ck0:ck0 + NCH * NMEM].rearrange("p (c t) -> p c t", c=NCH)
        t_ck = [T("ck%d" % c) for c in range(NCH)]
        cv = AB[:, cv0:cv0 + 2 * 512].rearrange("p (b d) -> p b d", b=2)
        t_cv = [T("cv%d" % c) for c in range(2)]
        pb_r = Ring([(AB[:, pb0 + i * TT: pb0 + (i + 1) * TT], T("pb%d" % i)) for i in range(3)])
        sq_rB = Ring([(AB[:, sqb0 + i * TT: sqb0 + (i + 1) * TT], T("sqb%d" % i)) for i in range(2)])
        f = 128
        x0, f = carve(f, ND * TT)
        rtb0, f = carve(f, TT)
        rlb0, f = carve(f, TT)
        tmp0, f = carve(f, 2 * TT)
        xsb0, f = carve(f, 2 * TT)
        assert f <= NF_ARENA, (f, NF_ARENA)
        xTt = AFp[:, x0:x0 + ND * TT].rearrange("p (c t) -> p c t", c=ND)
        t_x = [T("x%d" % c) for c in range(ND)]
        rtB, t_rtB = AFp[:, rtb0:rtb0 + TT], T("rtB")
        rlB, t_rlB = AFp[:, rlb0:rlb0 + TT], T("rlB")
        tmp_r = Ring([(AFp[:, tmp0 + i * TT: tmp0 + (i + 1) * TT], T("tmp%d" % i)) for i in range(2)])
        xs_rB = Ring([(AFp[:, xsb0 + i * TT: xsb0 + (i + 1) * TT], T("xsb%d" % i)) for i in range(2)])
        g_r = Ring([0, 1, 2, 3])
        OBk, LBk, RBk = 4, 5, 6

        mnT = actT
        t_mn = [[t_act[c]] for c in range(ND)]
        build_hT(memT, mnT, t_mn, G_MEM, 1, NMEM, xs_rB, sq_rB, rtB, t_rtB, RBk)

        def ck_body(i, slab, t_slab):
            b = g_r.next()
            for c in range(ND):
                P.op(PE, MM(banks[b][:, 0:NMEM], slab[:, c * 128:(c + 1) * 128], mnT[:, c, 0:NMEM],
                            start=(c == 0), stop=(c == ND - 1)), reads=[t_slab, t_act[c]], writes=[tbank[b]])
            P.op(ACT, AC(ckT[:, i, :], banks[b][:, 0:NMEM], AF.Copy), reads=[tbank[b]], writes=[t_ck[i]])
        run_slabs(slabsB, [(w_ck_r[i], ND * 128) for i in range(NCH)], ck_body)

        cvb = [g_r.next(), g_r.next()]

        def cv_body(i, slab, t_slab):
            sl = slab[:, 0:CVK * 512].rearrange("p (k n) -> p k n", k=CVK)
            for mb in range(2):
                for k in range(CVK):
                    c = i * CVK + k
                    P.op(PE, MM(banks[cvb[mb]][:, :], mnT[:, c, mb * 128:(mb + 1) * 128], sl[:, k, :],
                                start=(c == 0), stop=(c == ND - 1)), reads=[t_slab, t_act[c]], writes=[tbank[cvb[mb]]])
        run_slabs(slabsB, [(w_cv_r[i], CVK * 512) for i in range(NCV)], cv_body)
        for mb in range(2):
            P.op(ACT, AC(cv[:, mb, :], banks[cvb[mb]][:, :], AF.Copy), reads=[tbank[cvb[mb]]], writes=[t_cv[mb]])

        def stats_x():
            for c in range(ND):
                sq, t_sq = sq_rB.next()
                P.op(ACT, AC(sq, xTt[:, c, :], AF.Square), reads=[t_x[c]], writes=[t_sq])
                P.op(PE, MM(banks[RBk][:, :], ONES, sq, start=(c == 0), stop=(c == ND - 1)),
                     reads=[t_sq, t_cb], writes=[tbank[RBk]])
            rstd_from_bank(banks[RBk][:, :], tbank[RBk], rtB, t_rtB, 1.0 / D, eps6)

        def norm_to_act(g_off):
            stats_x()
            for c in range(ND):
                P.op(DVE, STT(actT[:, c, :], xTt[:, c, :], GV[:, g_off + c:g_off + c + 1], rtB, ALU.mult, ALU.mult),
                     reads=[t_x[c], t_rtB, t_gv], writes=[t_act[c]])

        def acc_x(b, oc):
            P.op(DVE, TTO(xTt[:, oc, :], banks[b][:, :], xTt[:, oc, :], ALU.add),
                 reads=[tbank[b], t_x[oc]], writes=[t_x[oc]])

        GX = max(1, ND // 4)
        t_xg = [T("xg%d" % g) for g in range(ND // GX)]
        t_ag = [T("ag%d" % g) for g in range(ND // GX)]
        def load_x(tt, g):
            c0, c1 = g * GX, (g + 1) * GX
            P.dma(SP, DM(xTt[:, c0:c1, :], xT_own[c0:c1, :, tt * TT:(tt + 1) * TT].rearrange("c p t -> p c t")),
                  t_xg[g], writes=t_x[c0:c1])

        def load_at(tt, g):
            c0, c1 = g * GX, (g + 1) * GX
            P.dma(SP, DM(actT[:, c0:c1, :], at_s[c0:c1, :, tt * TT:(tt + 1) * TT].rearrange("c p t -> p c t")),
                  t_ag[g], reads=t_at_s[c0:c1], writes=t_act[c0:c1])

        for tt in range(2):
            if tt == 0:
                for g in range(ND // GX):
                    load_at(tt, g)
                for g in range(ND // GX):
                    load_x(tt, g)

            def wout_body(oc, slab, t_slab):
                b = g_r.next()
                for kc in range(ND):
                    P.op(PE, MM(banks[b][:, :], slab[:, kc * 128:(kc + 1) * 128], actT[:, kc, :],
                                start=(kc == 0), stop=(kc == ND - 1)), reads=[t_slab, t_act[kc]], writes=[tbank[b]])
                acc_x(b, oc)
            run_slabs(slabsB, [(w_out_r[oc], ND * 128) for oc in range(ND)], wout_body)

            norm_to_act(G_CROSS)

            def cq_body(i, slab, t_slab):
                b = g_r.next()
                for c in range(ND):
                    P.op(PE, MM(banks[b][:, :], slab[:, c * 128:(c + 1) * 128], actT[:, c, :],
                                start=(c == 0), stop=(c == ND - 1)), reads=[t_slab, t_act[c]], writes=[tbank[b]])
                P.op(ACT, AC(cqT[:, i, :], banks[b][:, :], AF.Copy), reads=[tbank[b]], writes=[t_cq[i]])
            run_slabs(slabsB, [(w_cq_r[i], ND * 128) for i in range(NCH)], cq_body)

            for hh in range(NCH):
                for mb in range(2):
                    sb = g_r.next()
                    P.op(PE, MM(banks[sb][:, :], ckT[:, hh, mb * 128:(mb + 1) * 128], cqT[:, hh, :]),
                         reads=[t_ck[hh], t_cq[hh]], writes=[tbank[sb]])
                    pt, t_pt = pb_r.next()
                    P.op(ACT, AC(pt, banks[sb][:, :], AF.Exp, scale=scale), reads=[tbank[sb]], writes=[t_pt])
                    P.op(PE, MM(banks[OBk][:, :], cv[:, mb, hh * 128:(hh + 1) * 128], pt, start=(mb == 0), stop=(mb == 1)),
                         reads=[t_cv[mb], t_pt], writes=[tbank[OBk]])
                    P.op(PE, MM(banks[LBk][:, :], ONES, pt, start=(mb == 0), stop=(mb == 1)),
                         reads=[t_pt, t_cb], writes=[tbank[LBk]])
                P.op(DVE, RC(rlB, banks[LBk][:, :]), reads=[tbank[LBk]], writes=[t_rlB])
                P.op(DVE, TTO(coT[:, hh, :], banks[OBk][:, :], rlB, ALU.mult), reads=[tbank[OBk], t_rlB], writes=[t_co[hh]])

            def wco_body(g, slab, t_slab):
                sl = slab[:, 0:NCH * GC].rearrange("p (k n) -> p k n", k=NCH)
                for j in range(GC // 128):
                    oc = g * (GC // 128) + j
                    b = g_r.next()
                    for kc in range(NCH):
                        P.op(PE, MM(banks[b][:, :], sl[:, kc, j * 128:(j + 1) * 128], coT[:, kc, :],
                                    start=(kc == 0), stop=(kc == NCH - 1)), reads=[t_slab, t_co[kc]], writes=[tbank[b]])
                    acc_x(b, oc)
            run_slabs(slabsB, [(w_co_r[g], NCH * GC) for g in range(NGC)], wco_body)

            norm_to_act(G_MLP)
            srcs = []
            order = []
            for s in range(NSLAB + 1):
                if s < NSLAB:
                    for hcl in range(KS):
                        order.append(("up", s, hcl))
                        srcs.append((w_up_r[s * KS + hcl], ND * 128))
                if s >= 1:
                    for og in range(NOG):
                        order.append(("dn", s - 1, og))
                        srcs.append((w_dn_r[(s - 1) * NOG + og], KS * OG))

            def mlp_body(i, slab, t_slab):
                kind, s, j = order[i]
                u_ap, t_u = uT[s % 2]
                if kind == "up":
                    b = g_r.next()
                    for c in range(ND):
                        P.op(PE, MM(banks[b][:, :], slab[:, c * 128:(c + 1) * 128], actT[:, c, :],
                                    start=(c == 0), stop=(c == ND - 1)), reads=[t_slab, t_act[c]], writes=[tbank[b]])
                    tm, t_tm = tmp_r.next()
                    P.op(ACT, AC(tm, banks[b][:, :], AF.Relu), reads=[tbank[b]], writes=[t_tm])
                    P.op(DVE, TTO(u_ap[:, j, :], tm, tm, ALU.mult), reads=[t_tm], writes=[t_u[j]])
                else:
                    sl = slab[:, 0:KS * OG].rearrange("p (k n) -> p k n", k=KS)
                    for jj in range(OG // 128):
                        oc = j * (OG // 128) + jj
                        b = g_r.next()
                        for kc in range(KS):
                            P.op(PE, MM(banks[b][:, :], sl[:, kc, jj * 128:(jj + 1) * 128], u_ap[:, kc, :],
                                        start=(kc == 0), stop=(kc == KS - 1)), reads=[t_slab, t_u[kc]], writes=[tbank[b]])
                        acc_x(b, oc)
            run_slabs(slabsB, srcs, mlp_body)

            if tt + 1 < 2:
                for g in range(ND // GX):
                    load_at(tt + 1, g)
            stats_x()
            for c in range(ND):
                ys, t_ys = tmp_r.next()
                P.op(DVE, STT(ys, xTt[:, c, :], GV[:, G_FIN + c:G_FIN + c + 1], rtB, ALU.mult, ALU.mult),
                     reads=[t_x[c], t_rtB, t_gv], writes=[t_ys])
                P.dma(SP, DM(yT[c][:, tt * TT:(tt + 1) * TT], ys), t_ys, reads=[t_ys], writes=[T("y")])
                if tt + 1 < 2 and (c + 1) % GX == 0:
                    load_x(tt + 1, c // GX)
        P.fence()
        P.emit(nc)
    return nc


def _slabs_cols(w, ND, ncols_chunk=128):
    K, N = w.shape
    a = w.reshape(K // 128, 128, N // 128, 128)
    return np.ascontiguousarray(a.transpose(2, 1, 0, 3)).reshape(N // 128, 128, (K // 128) * 128)


def _slabs_rows(w, kper, ncol):
    K, N = w.shape
    ns = K // 128 // kper
    ng = N // ncol
    a = w.reshape(ns, kper, 128, ng, ncol)
    return np.ascontiguousarray(a.transpose(0, 3, 2, 1, 4)).reshape(ns * ng, 128, kper * ncol)


def _consts():
    ki = np.arange(128)[:, None]
    j = np.arange(19 * 128)[None, :]
    d = j - 384 - ki
    c = ((d >= 0) & (d <= 128)).astype(np.float32) + ((d >= 0) & (d <= 512) & (d % 4 == 0)) + \
        ((d >= 0) & (d <= 2048) & (d % 16 == 0))
    q = np.arange(128)[None, :]
    tri = (q >= ki).astype(np.float32)
    perm = np.zeros((128, 128), np.float32)
    perm[(np.arange(128) + 64) % 128, np.arange(128)] = 1.0
    ones = np.ones((128, 128), np.float32)
    ident = np.eye(128, dtype=np.float32)
    return np.concatenate([c, tri, perm, ones, ident], axis=1).astype(ml_dtypes.bfloat16)


def _rope_tables(pos):
    inv_freq = (10000.0 ** (-np.arange(0, HD, 2, dtype=np.float32) / HD)).astype(np.float32)
    ang = pos.astype(np.float32)[:, None] * inv_freq[None, :]
    cos = np.cos(ang).astype(np.float32)
    sin = np.sin(ang).astype(np.float32)
    cosT = np.concatenate([cos, cos], axis=1).T
    sinT = np.concatenate([-sin, sin], axis=1).T
    return np.ascontiguousarray(cosT), np.ascontiguousarray(sinT)


_PROG_CACHE = {}
_STOP = 99


def _run(inp, D, B):
    ND = D // 128
    f32 = np.float32
    x = np.asarray(inp["x"], f32)
    mem = np.asarray(inp["mem"], f32)
    KS = min(8, ND)
    OG = min(512, D)
    GC = min(1024, D)
    CVK = min(8, ND)
    w_in = np.asarray(inp["w_in"], f32)[0]
    w_ckv = np.asarray(inp["w_ckv"], f32)[0]
    shared = {
        "w_in_r": _slabs_cols(w_in, ND),
        "w_out_r": _slabs_cols(np.asarray(inp["w_out"], f32)[0], ND),
        "w_cq_r": _slabs_cols(np.asarray(inp["w_cq"], f32)[0], ND),
        "w_ck_r": _slabs_cols(np.ascontiguousarray(w_ckv[:, :512]), ND),
        "w_cv_r": _slabs_rows(np.ascontiguousarray(w_ckv[:, 512:]), CVK, 512),
        "w_co_r": _slabs_rows(np.asarray(inp["w_co"], f32)[0], NCH, GC),
        "w_up_r": _slabs_cols(np.asarray(inp["w_up"], f32)[0], ND),
        "w_dn_r": _slabs_rows(np.asarray(inp["w_down"], f32)[0], KS, OG),
        "cb": _consts(),
    }
    gv = np.concatenate([
        np.asarray(inp["norm_mix"], f32)[0].reshape(ND, 128).T,
        np.asarray(inp["norm_cross"], f32)[0].reshape(ND, 128).T,
        np.asarray(inp["norm_mem"], f32)[0].reshape(ND, 128).T,
        np.asarray(inp["norm_mlp"], f32)[0].reshape(ND, 128).T,
        np.asarray(inp["norm_final"], f32).reshape(ND, 128).T,
        np.asarray(inp["diff_subln"], f32)[0].reshape(2, 128).T,
    ], axis=1)
    shared["gv"] = np.ascontiguousarray(gv)
    shared["lp"] = np.ascontiguousarray(np.broadcast_to(np.asarray(inp["diff_lambda"], f32)[0].reshape(1, 512), (128, 512)))
    in_maps = []
    for core in range(2 * B):
        b, qh = core // 2, core % 2
        own = slice(qh * HALF, (qh + 1) * HALF)
        oth = slice((1 - qh) * HALF, (2 - qh) * HALF)
        m = dict(shared)
        m["xT_own"] = np.ascontiguousarray(x[b, own].T).reshape(ND, 128, HALF)
        m["xT_oth"] = np.ascontiguousarray(x[b, oth].T).reshape(ND, 128, HALF)
        m["memT"] = np.ascontiguousarray(mem[b].T).reshape(ND, 128, NMEM)
        m["cos_own"], m["sin_own"] = _rope_tables(np.arange(own.start, own.stop))
        m["cos_oth"], m["sin_oth"] = _rope_tables(np.arange(oth.start, oth.stop))
        m["visb"] = np.full((128, 1), 0.0 if qh == 1 else NEG, f32)
        in_maps.append(m)
    if D not in _PROG_CACHE:
        _PROG_CACHE[D] = build_program(D, _STOP)
    nc = _PROG_CACHE[D]
    res = run_bass_kernel_spmd(nc, in_maps, core_ids=list(range(2 * B)))
    out = np.empty((B, S, D), f32)
    for core in range(2 * B):
        b, qh = core // 2, core % 2
        out[b, qh * HALF:(qh + 1) * HALF] = np.asarray(res.results[core]["yT"]).reshape(D, HALF).T
    return out


def kernel(**inputs):
    return _run(inputs, 4096, 4)
```

```python
import math
from contextlib import ExitStack

import numpy as np
import ml_dtypes
import concourse.bass as bass
import concourse.mybir as mybir
from concourse.bass_utils import run_bass_kernel_spmd

F32 = mybir.dt.float32
BF16 = mybir.dt.bfloat16
AF = mybir.ActivationFunctionType
ALU = mybir.AluOpType
PE, ACT, DVE, POOL, SP = "pe", "act", "dve", "pool", "sp"

S = 2048
HALF = 1024
TT = 512
HD = 128
NMEM = 256
NCH = 4
LAMBDA_INIT = 0.8 - 0.6 * math.exp(-0.3 * 0)
NEG = -30000.0


class T:
    __slots__ = ("name", "writer", "readers", "sem", "dma_cnt")

    def __init__(self, name):
        self.name = name
        self.writer = None
        self.readers = []
        self.sem = None
        self.dma_cnt = 0


class Op:
    __slots__ = ("eng", "fn", "deps", "is_dma", "dst", "needs_inc", "tok")

    def __init__(self, eng, fn, is_dma=False, dst=None):
        self.eng = eng
        self.fn = fn
        self.deps = []
        self.is_dma = is_dma
        self.dst = dst
        self.needs_inc = False
        self.tok = None


class Prog:
    def __init__(self):
        self.ops = {PE: [], ACT: [], DVE: [], POOL: [], SP: []}
        self.dma_tiles = []
        self.last_dma = {}
        self.fence_ap = None

    def _link(self, op, deps):
        seen = set()
        for d in deps:
            if d is op or id(d) in seen:
                continue
            seen.add(id(d))
            if d.eng == PE and op.eng == PE and not d.is_dma and not op.is_dma:
                continue
            op.deps.append(d)
            if not d.is_dma:
                d.needs_inc = True

    def _add(self, op, reads, writes):
        deps = []
        for t in reads:
            if t.writer is not None:
                deps.append(t.writer)
        for t in writes:
            if t.writer is not None:
                deps.append(t.writer)
            deps.extend(t.readers)
        self._link(op, deps)
        for t in writes:
            t.writer = op
            t.readers = []
        for t in reads:
            if t.writer is not op:
                t.readers.append(op)
        self.ops[op.eng].append(op)
        return op

    def op(self, eng, fn, reads=(), writes=()):
        return self._add(Op(eng, fn), list(reads), list(writes))

    def dma(self, eng, fn, dst, reads=(), writes=()):
        if dst.sem is None:
            dst.sem = -1
            self.dma_tiles.append(dst)
        o = Op(eng, fn, is_dma=True, dst=dst)
        self._add(o, list(reads), list(writes))
        self.last_dma[id(dst)] = o
        return o

    def fence(self):
        deps = []
        for e in (PE, ACT, DVE, POOL):
            for o in reversed(self.ops[e]):
                if not o.is_dma and o.fn is not None:
                    deps.append(o)
                    break
        deps.extend(self.last_dma.values())
        self.last_dma = {}
        ap = self.fence_ap
        j = Op(POOL, lambda e: e.memset(ap, 0.0))
        self._link(j, deps)
        self.ops[POOL].append(j)
        j.needs_inc = True
        for e in (PE, ACT, DVE, SP):
            w = Op(e, None)
            w.deps.append(j)
            self.ops[e].append(w)

    def emit(self, nc):
        with ExitStack() as es:
            esem = {}
            for e in (PE, ACT, DVE, POOL):
                esem[e] = es.enter_context(nc.semaphore("s_" + e))
            for i, t in enumerate(self.dma_tiles):
                t.sem = es.enter_context(nc.semaphore("d%d" % i))
            for e, lst in self.ops.items():
                cnt = 0
                for o in lst:
                    if o.is_dma:
                        o.dst.dma_cnt += 1
                        o.tok = (o.dst.sem, 16 * o.dst.dma_cnt)
                    elif o.needs_inc:
                        cnt += 1
                        o.tok = (esem[e], cnt)
            block = es.enter_context(nc.Block())
            handles = {PE: block.tensor, ACT: block.scalar, DVE: block.vector,
                       POOL: block.gpsimd, SP: block.sync}
            for e, lst in self.ops.items():
                if not lst:
                    continue

                def body(eng, lst=lst):
                    known = {}
                    for o in lst:
                        for d in o.deps:
                            sem, val = d.tok
                            k = id(sem)
                            if known.get(k, 0) >= val:
                                continue
                            known[k] = val
                            eng.wait_ge(sem, val)
                        if o.fn is None:
                            continue
                        ins = o.fn(eng)
                        if o.tok is not None:
                            ins.then_inc(o.tok[0], 16 if o.is_dma else 1)
                handles[e](body)


class Ring:
    def __init__(self, items):
        self.items = items
        self.i = 0

    def next(self):
        it = self.items[self.i % len(self.items)]
        self.i += 1
        return it


def MM(out, lhsT, rhs, start=True, stop=True):
    return lambda e: e.matmul(out, lhsT=lhsT, rhs=rhs, start=start, stop=stop)


def TR(out, in_, ident):
    return lambda e: e.transpose(out, in_, ident)


def AC(out, in_, func, bias=None, scale=None):
    kw = {}
    if bias is not None:
        kw["bias"] = bias
    if scale is not None:
        kw["scale"] = scale
    return lambda e: e.activation(out=out, in_=in_, func=func, **kw)


def TTO(out, in0, in1, op):
    return lambda e: e.tensor_tensor(out=out, in0=in0, in1=in1, op=op)


def STT(out, in0, scalar, in1, op0, op1):
    return lambda e: e.scalar_tensor_tensor(out=out, in0=in0, scalar=scalar, in1=in1, op0=op0, op1=op1)


def TS(out, in0, s1, op0, op1=None, accum_out=None):
    kw = {}
    if op1 is not None:
        kw["op1"] = op1
    if accum_out is not None:
        kw["accum_out"] = accum_out
    return lambda e: e.tensor_scalar(out=out, in0=in0, scalar1=s1, scalar2=None, op0=op0, **kw)


def RC(out, in_):
    return lambda e: e.reciprocal(out=out, in_=in_)


def CP(out, in_):
    return lambda e: e.tensor_copy(out=out, in_=in_)


def MS(ap, v):
    return lambda e: e.memset(ap, v)


def DM(out, in_, cast=False):
    if cast:
        return lambda e: e.dma_start(out=out, in_=in_, max_dma_last_dim=8192)
    return lambda e: e.dma_start(out=out, in_=in_)


def build_program(D, stop=99):
    ND = D // 128
    NQ = D // 256
    HDIFF = D // 512
    HDIL = D // 256
    DFF = 4 * D
    NFF = DFF // 128
    KS = min(8, ND)
    NSLAB = NFF // KS
    OG = min(512, D)
    NOG = D // OG
    GC = min(1024, D)
    NGC = D // GC
    CVK = min(8, ND)
    NCV = ND // CVK
    SLABE = 4096
    assert ND * 128 <= SLABE and KS * OG <= SLABE and NCH * GC <= SLABE and CVK * 512 <= SLABE

    nc = bass.Bass("TRN2", target_bir_lowering=False)

    def din(name, shape, dt=F32):
        return nc.dram_tensor(name, list(shape), dt, kind="ExternalInput").ap()

    xT_own = din("xT_own", [ND, 128, HALF])
    xT_oth = din("xT_oth", [ND, 128, HALF])
    memT = din("memT", [ND, 128, NMEM])
    cos_own = din("cos_own", [128, HALF])
    sin_own = din("sin_own", [128, HALF])
    cos_oth = din("cos_oth", [128, HALF])
    sin_oth = din("sin_oth", [128, HALF])
    visb_d = din("visb", [128, 1])
    cb_d = din("cb", [128, 23 * 128], BF16)
    gv_d = din("gv", [128, 5 * ND + 2])
    lp_d = din("lp", [128, 4 * 128])
    w_in_r = din("w_in_r", [3 * D // 128, 128, ND * 128])
    w_out_r = din("w_out_r", [ND, 128, ND * 128])
    w_cq_r = din("w_cq_r", [NCH, 128, ND * 128])
    w_ck_r = din("w_ck_r", [NCH, 128, ND * 128])
    w_cv_r = din("w_cv_r", [NCV, 128, CVK * 512])
    w_co_r = din("w_co_r", [NGC, 128, NCH * GC])
    w_up_r = din("w_up_r", [NFF, 128, ND * 128])
    w_dn_r = din("w_dn_r", [NSLAB * NOG, 128, KS * OG])
    yT = nc.dram_tensor("yT", [ND, 128, HALF], F32, kind="ExternalOutput").ap()
    qT_s = nc.dram_tensor("qT_s", [2 * NQ, 128, HALF], BF16).ap()
    kT_s = nc.dram_tensor("kT_s", [2 * NQ, 128, S], BF16).ap()
    v_s = nc.dram_tensor("v_s", [2 * NQ, 128, 16 * 128], BF16).ap()
    at_s = nc.dram_tensor("at_s", [ND, 128, HALF], BF16).ap()

    NB_ARENA = max(ND * HALF + 4 * SLABE + 7 * TT, ND * TT + 4 * SLABE + 2 * KS * TT + 8 * TT + 2048 + 5 * TT, 26624)
    NF_ARENA = max(128 + ND * TT + 6 * TT, 128 + 2 * HALF + 17 * TT)

    with ExitStack() as es:
        AB = es.enter_context(nc.sbuf_tensor("arena_b", [128, NB_ARENA], BF16))
        AFp = es.enter_context(nc.sbuf_tensor("arena_f", [128, NF_ARENA], F32))
        CB = es.enter_context(nc.sbuf_tensor("cb_sb", [128, 23 * 128], BF16))
        GV = es.enter_context(nc.sbuf_tensor("gv_sb", [128, 5 * ND + 2], F32))
        SM = es.enter_context(nc.sbuf_tensor("small", [128, 16], F32))
        LP = es.enter_context(nc.sbuf_tensor("lp_sb", [128, 4 * 128], F32))
        banks = [es.enter_context(nc.psum_tensor("bank%d" % i, [128, 512], F32)) for i in range(7)]
        bankT = es.enter_context(nc.psum_tensor("bankT", [128, 1024], BF16))
        banks.append(None)
        tbank = [T("bank%d" % i) for i in range(8)]

        P = Prog()
        P.fence_ap = SM[:, 15:16]

        Mmask = CB[:, 0:19 * 128]
        TRI = CB[:, 19 * 128:20 * 128]
        PERM = CB[:, 20 * 128:21 * 128]
        ONES = CB[:, 21 * 128:22 * 128]
        IDENT = CB[:, 22 * 128:23 * 128]
        t_cb, t_gv, t_lp = T("cb"), T("gv"), T("lp")
        G_MIX, G_CROSS, G_MEM, G_MLP, G_FIN, G_SUB = 0, ND, 2 * ND, 3 * ND, 4 * ND, 5 * ND
        visb = SM[:, 0:1]
        neglam = SM[:, 1:2]
        eps6 = SM[:, 7:8]
        eps5 = SM[:, 8:9]
        gsub = SM[:, 9:11]

        P.dma(SP, DM(CB[:], cb_d), t_cb, writes=[t_cb])
        P.dma(SP, DM(GV[:], gv_d), t_gv, writes=[t_gv])
        P.dma(SP, DM(LP[:], lp_d), t_lp, writes=[t_lp])
        t_vis = T("vis")
        P.dma(SP, DM(visb, visb_d), t_vis, writes=[t_vis])
        t_s1, t_s2, t_e, t_d, t_nl, t_eps, t_gs = T("s1"), T("s2"), T("e"), T("d"), T("nl"), T("eps"), T("gs")
        junk = AFp[:, 0:128]
        t_junk = T("junk")
        P.op(DVE, TTO(junk, LP[:, 0:128], LP[:, 128:256], ALU.mult), reads=[t_lp], writes=[t_junk])
        P.op(DVE, TS(junk, junk, 1.0, ALU.mult, op1=ALU.add, accum_out=SM[:, 2:3]), reads=[t_junk], writes=[t_junk, t_s1])
        P.op(DVE, TTO(junk, LP[:, 256:384], LP[:, 384:512], ALU.mult), reads=[t_lp], writes=[t_junk])
        P.op(DVE, TS(junk, junk, 1.0, ALU.mult, op1=ALU.add, accum_out=SM[:, 3:4]), reads=[t_junk], writes=[t_junk, t_s2])
        P.op(ACT, AC(SM[:, 4:6], SM[:, 2:4], AF.Exp), reads=[t_s1, t_s2], writes=[t_e])
        P.op(DVE, TTO(SM[:, 6:7], SM[:, 5:6], SM[:, 4:5], ALU.subtract), reads=[t_e], writes=[t_d])
        P.op(DVE, TS(neglam, SM[:, 6:7], -LAMBDA_INIT, ALU.add), reads=[t_d], writes=[t_nl])
        P.op(DVE, MS(eps6, 1e-6), writes=[t_eps])
        P.op(DVE, MS(eps5, 1e-5), writes=[t_eps])
        P.op(DVE, TS(gsub, GV[:, G_SUB:G_SUB + 2], 1.0 - LAMBDA_INIT, ALU.mult), reads=[t_gv], writes=[t_gs])

        def carve(base, n):
            return base, base + n

        def rstd_from_bank(bk_ap, tb_, rt, t_rt_, inv_n, eps_ap):
            P.op(ACT, AC(rt, bk_ap, AF.Sqrt, bias=eps_ap, scale=inv_n), reads=[tb_, t_eps], writes=[t_rt_])
            P.op(DVE, RC(rt, rt), reads=[t_rt_], writes=[t_rt_])

        def run_slabs(slots, srcs, body):
            look = len(slots) - 1
            q = []
            nxt = 0
            for i in range(len(srcs)):
                while nxt < len(srcs) and nxt <= i + look:
                    ap, t = slots[run_slabs.n % len(slots)]
                    run_slabs.n += 1
                    src_ap, nel = srcs[nxt]
                    P.dma(POOL, DM(ap[:, 0:nel], src_ap, cast=True), t, writes=[t])
                    q.append((ap, t))
                    nxt += 1
                ap, t = q.pop(0)
                body(i, ap, t)
        run_slabs.n = 0

        def build_hT(xsrc, dst, t_dst, g_off, ntile, width, xs_ring, sq_ring, rt, t_rt_, rbank):
            for t in range(ntile):
                for c in range(ND):
                    xs, t_xs = xs_ring.next()
                    sq, t_sq = sq_ring.next()
                    P.dma(SP, DM(xs[:, 0:width], xsrc[c][:, t * width:(t + 1) * width]), t_xs, writes=[t_xs])
                    P.op(ACT, AC(sq[:, 0:width], xs[:, 0:width], AF.Square), reads=[t_xs], writes=[t_sq])
                    P.op(PE, MM(banks[rbank][:, 0:width], ONES, sq[:, 0:width], start=(c == 0), stop=(c == ND - 1)),
                         reads=[t_sq, t_cb], writes=[tbank[rbank]])
                rstd_from_bank(banks[rbank][:, 0:width], tbank[rbank], rt[:, 0:width], t_rt_, 1.0 / D, eps6)
                for c in range(ND):
                    xs, t_xs = xs_ring.next()
                    P.dma(SP, DM(xs[:, 0:width], xsrc[c][:, t * width:(t + 1) * width]), t_xs, writes=[t_xs])
                    P.op(DVE, STT(dst[:, c, t * width:(t + 1) * width], xs[:, 0:width], GV[:, g_off + c:g_off + c + 1],
                                  rt[:, 0:width], ALU.mult, ALU.mult),
                         reads=[t_xs, t_rt_, t_gv], writes=[t_dst[c][t]])

        if stop == 0:
            P.fence()
            P.emit(nc)
            return nc
        o = 0
        hT0, o = carve(o, ND * HALF)
        slab0, o = carve(o, 4 * SLABE)
        sq0, o = carve(o, 2 * TT)
        tb0, o = carve(o, 2 * TT)
        st0, o = carve(o, 3 * TT)
        assert o <= NB_ARENA, (o, NB_ARENA)
        hT = AB[:, hT0:hT0 + ND * HALF].rearrange("p (c t) -> p c t", c=ND)
        t_h = [[T("h%d_%d" % (c, t)) for t in range(2)] for c in range(ND)]
        slabsA = [(AB[:, slab0 + i * SLABE: slab0 + (i + 1) * SLABE], T("slab%d" % i)) for i in range(4)]
        sq_rA = Ring([(AB[:, sq0 + i * TT: sq0 + (i + 1) * TT], T("sq%d" % i)) for i in range(2)])
        tb_r = Ring([(AB[:, tb0 + i * TT: tb0 + (i + 1) * TT], T("tb%d" % i)) for i in range(2)])
        st_r = Ring([(AB[:, st0 + i * TT: st0 + (i + 1) * TT], T("st%d" % i)) for i in range(3)])
        f = 128
        cosT, f = carve(f, HALF)
        sinT, f = carve(f, HALF)
        xs0, f = carve(f, 12 * TT)
        rt0, f = carve(f, TT)
        t10, f = carve(f, 2 * TT)
        t20, f = carve(f, 2 * TT)
        assert f <= NF_ARENA, (f, NF_ARENA)
        cos_sb = AFp[:, cosT:cosT + HALF]
        sin_sb = AFp[:, sinT:sinT + HALF]
        t_cos, t_sin = T("cos"), T("sin")
        xs_rA = Ring([(AFp[:, xs0 + i * TT: xs0 + (i + 1) * TT], T("xs%d" % i)) for i in range(12)])
        rtA, t_rtA = AFp[:, rt0:rt0 + TT], T("rt")
        t1_r = Ring([(AFp[:, t10 + i * TT: t10 + (i + 1) * TT], T("t1_%d" % i)) for i in range(2)])
        t2_r = Ring([(AFp[:, t20 + i * TT: t20 + (i + 1) * TT], T("t2_%d" % i)) for i in range(2)])
        proj_r = Ring([0, 1, 2, 6])
        RB = 3
        pp_r = Ring([4, 5])
        tr_r = Ring([7])
        t_q_s = [T("q_s%d" % i) for i in range(2 * NQ)]
        t_k_s = [T("k_s%d" % i) for i in range(2 * NQ)]
        t_v_s = [T("v_s%d" % i) for i in range(2 * NQ)]
        t_at_s = [T("at_s%d" % i) for i in range(ND)]

        def chunk_kind(cc):
            r = cc // NQ
            j = cc % NQ
            return ("q", "k", "v", "q", "k", "v")[r], (j if r < 3 else NQ + j)

        for pas in (0, 1):
            own = (pas == 0)
            xsrc = xT_own if own else xT_oth
            P.dma(SP, DM(cos_sb, cos_own if own else cos_oth), t_cos, writes=[t_cos])
            P.dma(SP, DM(sin_sb, sin_own if own else sin_oth), t_sin, writes=[t_sin])
            build_hT(xsrc, hT, t_h, G_MIX, 2, TT, xs_rA, sq_rA, rtA, t_rtA, RB)
            chunks = [cc for cc in range(3 * D // 128) if own or chunk_kind(cc)[0] != "q"]
            kvoff = HALF if own else 0

            pending = []

            def flush():
                while pending:
                    pending.pop(0)()

            def body(i, slab, t_slab, chunks=chunks, own=own, kvoff=kvoff):
                cc = chunks[i]
                kind, idx = chunk_kind(cc)
                for t in range(2):
                    b = proj_r.next()
                    for c in range(ND):
                        P.op(PE, MM(banks[b][:, :], slab[:, c * 128:(c + 1) * 128], hT[:, c, t * TT:(t + 1) * TT],
                                    start=(c == 0), stop=(c == ND - 1)),
                             reads=[t_slab, t_h[c][t]], writes=[tbank[b]])
                    flush()
                    tb, t_tb = tb_r.next()
                    P.op(ACT, AC(tb, banks[b][:, :], AF.Copy), reads=[tbank[b]], writes=[t_tb])
                    pending.append(lambda b=b, tb=tb, t_tb=t_tb, t=t, kind=kind, idx=idx: post(b, tb, t_tb, t, kind, idx))

            def post(b, tb, t_tb, t, kind, idx, own=own, kvoff=kvoff):
                if True:
                    st, t_st = st_r.next()
                    if kind == "v":
                        tr = tr_r.next()
                        trb = bankT
                        for j in range(4):
                            P.op(PE, TR(trb[:, j * 128:(j + 1) * 128], tb[:, j * 128:(j + 1) * 128], IDENT),
                                 reads=[t_tb, t_cb], writes=[tbank[tr]])
                        P.op(DVE, CP(st, trb[:, 0:512]), reads=[tbank[tr]], writes=[t_st])
                        blk0 = (8 if own else 0) + t * 4
                        P.dma(SP, DM(v_s[idx][:, blk0 * 128:(blk0 + 4) * 128], st), t_st,
                              reads=[t_st], writes=[t_v_s[idx]])
                    else:
                        pp = pp_r.next()
                        P.op(PE, MM(banks[pp][:, :], PERM, tb), reads=[t_tb, t_cb], writes=[tbank[pp]])
                        t1, t_t1 = t1_r.next()
                        t2, t_t2 = t2_r.next()
                        P.op(DVE, TTO(t1, banks[b][:, :], cos_sb[:, t * TT:(t + 1) * TT], ALU.mult),
                             reads=[tbank[b], t_cos, t_tb], writes=[t_t1])
                        P.op(DVE, TTO(t2, banks[pp][:, :], sin_sb[:, t * TT:(t + 1) * TT], ALU.mult),
                             reads=[tbank[pp], t_sin], writes=[t_t2])
                        P.op(DVE, TTO(st, t1, t2, ALU.add), reads=[t_t1, t_t2], writes=[t_st])
                        if kind == "q":
                            P.dma(SP, DM(qT_s[idx][:, t * TT:(t + 1) * TT], st), t_st, reads=[t_st], writes=[t_q_s[idx]])
                        else:
                            P.dma(SP, DM(kT_s[idx][:, kvoff + t * TT: kvoff + (t + 1) * TT], st), t_st,
                                  reads=[t_st], writes=[t_k_s[idx]])

            run_slabs(slabsA, [(w_in_r[cc], ND * 128) for cc in chunks], body)
            flush()

        if stop == 1:
            P.fence()
            P.emit(nc)
            return nc
        P.fence()
        HBE = 2 * HALF + 2 * S + 2 * 16 * 128
        o = 0
        hb0, o = carve(o, 2 * HBE)
        p0, o = carve(o, 8 * TT)
        as0, o = carve(o, 2 * TT)
        sqd0, o = carve(o, 2 * TT)
        assert o <= NB_ARENA, (o, NB_ARENA)
        hbufs = []
        for i in range(2):
            base = hb0 + i * HBE
            q_ap = AB[:, base: base + 2 * HALF].rearrange("p (c t) -> p c t", c=2)
            k_ap = AB[:, base + 2 * HALF: base + 2 * HALF + 2 * S].rearrange("p (c t) -> p c t", c=2)
            v_ap = AB[:, base + 2 * HALF + 2 * S: base + HBE].rearrange("p (c b d) -> p c b d", c=2, b=16)
            hbufs.append((q_ap, k_ap, v_ap, [T("hq%d_%d" % (i, c)) for c in range(2)],
                          [T("hk%d_%d" % (i, c)) for c in range(2)], [T("hv%d_%d" % (i, c)) for c in range(2)]))
        p_r = Ring([(AB[:, p0 + i * TT: p0 + (i + 1) * TT], T("p%d" % i)) for i in range(8)])
        as_r = Ring([(AB[:, as0 + i * TT: as0 + (i + 1) * TT], T("as%d" % i)) for i in range(2)])
        sqd = [(AB[:, sqd0 + i * TT: sqd0 + (i + 1) * TT], T("sqd%d" % i)) for i in range(2)]
        f = 128
        rl0, f = carve(f, TT)
        on0, f = carve(f, 2 * TT)
        dd0, f = carve(f, 2 * TT)
        o2_0, f = carve(f, TT)
        rs0, f = carve(f, TT)
        osb0, f = carve(f, 2 * TT)
        lsb0, f = carve(f, TT)
        assert f <= NF_ARENA, (f, NF_ARENA)
        lsb, t_lsb = AFp[:, lsb0:lsb0 + TT], T("lsb")
        osb = [(AFp[:, osb0 + i * TT: osb0 + (i + 1) * TT], T("osb%d" % i)) for i in range(2)]
        rlA, t_rlA = AFp[:, rl0:rl0 + TT], T("rl")
        on = [(AFp[:, on0 + i * TT: on0 + (i + 1) * TT], T("on%d" % i)) for i in range(2)]
        dd = [(AFp[:, dd0 + i * TT: dd0 + (i + 1) * TT], T("dd%d" % i)) for i in range(2)]
        o2_ap, t_o2 = AFp[:, o2_0:o2_0 + TT], T("o2")
        rs_ap, t_rs = AFp[:, rs0:rs0 + TT], T("rs")
        OB = [0, 1]
        LB = 2
        s_r_diff = Ring([3, 4, 5])
        s_r_dil = Ring([1, 3, 4, 5])
        scale = 1.0 / math.sqrt(HD)

        jobs = [("diff", h) for h in range(HDIFF)] + [("dil", h) for h in range(HDIL)]

        def load_head(ji):
            kind, h = jobs[ji]
            q_ap, k_ap, v_ap, tq, tk, tv = hbufs[ji % 2]
            ncomp = 2 if kind == "diff" else 1
            for c in range(ncomp):
                qi = (2 * h + c) if kind == "diff" else (NQ + h)
                P.dma(SP, DM(q_ap[:, c, :], qT_s[qi]), tq[c], reads=[t_q_s[qi]], writes=[tq[c]])
                P.dma(SP, DM(k_ap[:, c, :], kT_s[qi]), tk[c], reads=[t_k_s[qi]], writes=[tk[c]])
                P.dma(SP, DM(v_ap[:, c, :, :], v_s[qi].rearrange("p (b d) -> p b d", b=16)), tv[c],
                      reads=[t_v_s[qi]], writes=[tv[c]])

        tail_pending = []
        norm_pending = []

        def attend(kind, h, q_ap, k_ap, v_ap, tq, tk, tv):
            ncomp = 2 if kind == "diff" else 1
            nvo = ncomp
            s_r = s_r_diff if kind == "diff" else s_r_dil
            LA = len(s_r.items) - 1
            for qt in range(2):
                qb0 = 8 + qt * 4
                blocks = list(range(0, qb0 + 4))
                for comp in range(ncomp):
                    sb_of = {}

                    def qk(kb):
                        c0 = max(0, kb - qb0) * 128
                        sb = s_r.next()
                        sb_of[kb] = sb
                        P.op(PE, MM(banks[sb][:, c0:TT], k_ap[:, comp, kb * 128:(kb + 1) * 128],
                                    q_ap[:, comp, qt * TT + c0:(qt + 1) * TT]),
                             reads=[tk[comp], tq[comp]], writes=[tbank[sb]])

                    for j in range(min(LA, len(blocks))):
                        qk(blocks[j])
                    for i, kb in enumerate(blocks):
                        if i == 3:
                            while norm_pending:
                                norm_pending.pop(0)()
                        if i == 8:
                            while tail_pending:
                                tail_pending.pop(0)()
                        if i + LA < len(blocks):
                            qk(blocks[i + LA])
                        c0 = max(0, kb - qb0) * 128
                        sb = sb_of[kb]
                        pt, t_pt = p_r.next()
                        if kb < 8:
                            P.op(ACT, AC(pt[:, c0:TT], banks[sb][:, c0:TT], AF.Exp, bias=visb, scale=scale),
                                 reads=[tbank[sb], t_vis], writes=[t_pt])
                        else:
                            P.op(ACT, AC(pt[:, c0:TT], banks[sb][:, c0:TT], AF.Exp, scale=scale),
                                 reads=[tbank[sb]], writes=[t_pt])
                        if kind == "dil":
                            mo = (qb0 - kb + 3) * 128 + c0
                            P.op(DVE, TTO(pt[:, c0:TT], pt[:, c0:TT], Mmask[:, mo:mo + TT - c0], ALU.mult),
                                 reads=[t_pt, t_cb], writes=[t_pt])
                        elif kb >= qb0:
                            P.op(DVE, TTO(pt[:, c0:c0 + 128], pt[:, c0:c0 + 128], TRI, ALU.mult),
                                 reads=[t_pt, t_cb], writes=[t_pt])
                        first = (i == 0)
                        last = (i == len(blocks) - 1)
                        for oc in range(nvo):
                            P.op(PE, MM(banks[OB[oc]][:, c0:TT], v_ap[:, oc, kb, :], pt[:, c0:TT], start=first, stop=last),
                                 reads=[tv[oc], t_pt], writes=[tbank[OB[oc]]])
                        P.op(PE, MM(banks[LB][:, c0:TT], ONES, pt[:, c0:TT], start=first, stop=last),
                             reads=[t_pt, t_cb], writes=[tbank[LB]])
                    P.op(DVE, CP(lsb, banks[LB][:, :]), reads=[tbank[LB]], writes=[t_lsb])
                    P.op(ACT, AC(osb[0][0], banks[OB[0]][:, :], AF.Copy), reads=[tbank[OB[0]]], writes=[osb[0][1]])
                    if nvo == 2:
                        P.op(DVE, CP(osb[1][0], banks[OB[1]][:, :]), reads=[tbank[OB[1]]], writes=[osb[1][1]])
                    def norm(kind=kind, comp=comp, h=h, qt=qt):
                        P.op(DVE, RC(rlA, lsb), reads=[t_lsb], writes=[t_rlA])
                        if kind == "dil":
                            st, t_st = as_r.next()
                            P.op(DVE, TTO(st, osb[0][0], rlA, ALU.mult), reads=[osb[0][1], t_rlA], writes=[t_st])
                            ch = D // 256 + h
                            P.dma(SP, DM(at_s[ch][:, qt * TT:(qt + 1) * TT], st), t_st, reads=[t_st], writes=[t_at_s[ch]])
                        elif comp == 0:
                            for oc in range(2):
                                P.op(DVE, TTO(on[oc][0], osb[oc][0], rlA, ALU.mult),
                                     reads=[osb[oc][1], t_rlA], writes=[on[oc][1]])
                        else:
                            for oc in range(2):
                                P.op(DVE, TTO(o2_ap, osb[oc][0], rlA, ALU.mult),
                                     reads=[osb[oc][1], t_rlA], writes=[t_o2])
                                P.op(DVE, STT(dd[oc][0], o2_ap, neglam, on[oc][0], ALU.mult, ALU.add),
                                     reads=[t_o2, t_nl, on[oc][1]], writes=[dd[oc][1]])
                                P.op(DVE, TTO(sqd[oc][0], dd[oc][0], dd[oc][0], ALU.mult), reads=[dd[oc][1]], writes=[sqd[oc][1]])

                            def tail(h=h, qt=qt):
                                RSB = 6
                                for oc in range(2):
                                    P.op(PE, MM(banks[RSB][:, :], ONES, sqd[oc][0], start=(oc == 0), stop=(oc == 1)),
                                         reads=[sqd[oc][1], t_cb], writes=[tbank[RSB]])
                                P.op(ACT, AC(rs_ap, banks[RSB][:, :], AF.Ln, bias=eps5, scale=1.0 / 256.0),
                                     reads=[tbank[RSB], t_eps], writes=[t_rs])
                                P.op(ACT, AC(rs_ap, rs_ap, AF.Exp, scale=-0.5), reads=[t_rs], writes=[t_rs])
                                for oc in range(2):
                                    st, t_st = as_r.next()
                                    P.op(DVE, STT(st, dd[oc][0], gsub[:, oc:oc + 1], rs_ap, ALU.mult, ALU.mult),
                                         reads=[dd[oc][1], t_gs, t_rs], writes=[t_st])
                                    ch = 2 * h + oc
                                    P.dma(SP, DM(at_s[ch][:, qt * TT:(qt + 1) * TT], st), t_st, reads=[t_st], writes=[t_at_s[ch]])
                            tail_pending.append(tail)
                    norm_pending.append(norm)

        load_head(0)
        for ji, (kind, h) in enumerate(jobs):
            if ji + 1 < len(jobs):
                load_head(ji + 1)
            attend(kind, h, *hbufs[ji % 2])
        while norm_pending:
            norm_pending.pop(0)()
        while tail_pending:
            tail_pending.pop(0)()

        if stop == 2:
            P.fence()
            P.emit(nc)
            return nc
        P.fence()
        o = 0
        act0, o = carve(o, ND * TT)
        slb0, o = carve(o, 4 * SLABE)
        ut0, o = carve(o, 2 * KS * TT)
        cq0, o = carve(o, NCH * TT)
        co0, o = carve(o, NCH * TT)
        ck0, o = carve(o, NCH * NMEM)
        cv0, o = carve(o, 2 * 512)
        pb0, o = carve(o, 3 * TT)
        sqb0, o = carve(o, 2 * TT)
        assert o <= NB_ARENA, (o, NB_ARENA)
        actT = AB[:, act0:act0 + ND * TT].rearrange("p (c t) -> p c t", c=ND)
        t_act = [T("act%d" % c) for c in range(ND)]
        slabsB = [(AB[:, slb0 + i * SLABE: slb0 + (i + 1) * SLABE], T("slabB%d" % i)) for i in range(4)]
        uT = [(AB[:, ut0 + i * KS * TT: ut0 + (i + 1) * KS * TT].rearrange("p (c t) -> p c t", c=KS),
               [T("u%d_%d" % (i, c)) for c in range(KS)]) for i in range(2)]
        cqT = AB[:, cq0:cq0 + NCH * TT].rearrange("p (c t) -> p c t", c=NCH)
        t_cq = [T("cq%d" % c) for c in range(NCH)]
        coT = AB[:, co0:co0 + NCH * TT].rearrange("p (c t) -> p c t", c=NCH)
        t_co = [T("co%d" % c) for c in range(NCH)]
        ckT = AB[:, ck0:ck0 + NCH * NMEM].rearrange("p (c t) -> p c t", c=NCH)
        t_ck = [T("ck%d" % c) for c in range(NCH)]
        cv = AB[:, cv0:cv0 + 2 * 512].rearrange("p (b d) -> p b d", b=2)
        t_cv = [T("cv%d" % c) for c in range(2)]
        pb_r = Ring([(AB[:, pb0 + i * TT: pb0 + (i + 1) * TT], T("pb%d" % i)) for i in range(3)])
        sq_rB = Ring([(AB[:, sqb0 + i * TT: sqb0 + (i + 1) * TT], T("sqb%d" % i)) for i in range(2)])
        f = 128
        x0, f = carve(f, ND * TT)
        rtb0, f = carve(f, TT)
        rlb0, f = carve(f, TT)
        tmp0, f = carve(f, 2 * TT)
        xsb0, f = carve(f, 2 * TT)
        assert f <= NF_ARENA, (f, NF_ARENA)
        xTt = AFp[:, x0:x0 + ND * TT].rearrange("p (c t) -> p c t", c=ND)
        t_x = [T("x%d" % c) for c in range(ND)]
        rtB, t_rtB = AFp[:, rtb0:rtb0 + TT], T("rtB")
        rlB, t_rlB = AFp[:, rlb0:rlb0 + TT], T("rlB")
        tmp_r = Ring([(AFp[:, tmp0 + i * TT: tmp0 + (i + 1) * TT], T("tmp%d" % i)) for i in range(2)])
        xs_rB = Ring([(AFp[:, xsb0 + i * TT: xsb0 + (i + 1) * TT], T("xsb%d" % i)) for i in range(2)])
        g_r = Ring([0, 1, 2, 3])
        OBk, LBk, RBk = 4, 5, 6

        mnT = actT
        t_mn = [[t_act[c]] for c in range(ND)]
        build_hT(memT, mnT, t_mn, G_MEM, 1, NMEM, xs_rB, sq_rB, rtB, t_rtB, RBk)

        def ck_body(i, slab, t_slab):
            b = g_r.next()
            for c in range(ND):
                P.op(PE, MM(banks[b][:, 0:NMEM], slab[:, c * 128:(c + 1) * 128], mnT[:, c, 0:NMEM],
                            start=(c == 0), stop=(c == ND - 1)), reads=[t_slab, t_act[c]], writes=[tbank[b]])
            P.op(ACT, AC(ckT[:, i, :], banks[b][:, 0:NMEM], AF.Copy), reads=[tbank[b]], writes=[t_ck[i]])
        run_slabs(slabsB, [(w_ck_r[i], ND * 128) for i in range(NCH)], ck_body)

        cvb = [g_r.next(), g_r.next()]

        def cv_body(i, slab, t_slab):
            sl = slab[:, 0:CVK * 512].rearrange("p (k n) -> p k n", k=CVK)
            for mb in range(2):
                for k in range(CVK):
                    c = i * CVK + k
                    P.op(PE, MM(banks[cvb[mb]][:, :], mnT[:, c, mb * 128:(mb + 1) * 128], sl[:, k, :],
                                start=(c == 0), stop=(c == ND - 1)), reads=[t_slab, t_act[c]], writes=[tbank[cvb[mb]]])
        run_slabs(slabsB, [(w_cv_r[i], CVK * 512) for i in range(NCV)], cv_body)
        for mb in range(2):
            P.op(ACT, AC(cv[:, mb, :], banks[cvb[mb]][:, :], AF.Copy), reads=[tbank[cvb[mb]]], writes=[t_cv[mb]])

        def stats_x():
            for c in range(ND):
                sq, t_sq = sq_rB.next()
                P.op(ACT, AC(sq, xTt[:, c, :], AF.Square), reads=[t_x[c]], writes=[t_sq])
                P.op(PE, MM(banks[RBk][:, :], ONES, sq, start=(c == 0), stop=(c == ND - 1)),
                     reads=[t_sq, t_cb], writes=[tbank[RBk]])
            rstd_from_bank(banks[RBk][:, :], tbank[RBk], rtB, t_rtB, 1.0 / D, eps6)

        def norm_to_act(g_off):
            stats_x()
            for c in range(ND):
                P.op(DVE, STT(actT[:, c, :], xTt[:, c, :], GV[:, g_off + c:g_off + c + 1], rtB, ALU.mult, ALU.mult),
                     reads=[t_x[c], t_rtB, t_gv], writes=[t_act[c]])

        def acc_x(b, oc):
            P.op(DVE, TTO(xTt[:, oc, :], banks[b][:, :], xTt[:, oc, :], ALU.add),
                 reads=[tbank[b], t_x[oc]], writes=[t_x[oc]])

        GX = max(1, ND // 4)
        t_xg = [T("xg%d" % g) for g in range(ND // GX)]
        t_ag = [T("ag%d" % g) for g in range(ND // GX)]
        def load_x(tt, g):
            c0, c1 = g * GX, (g + 1) * GX
            P.dma(SP, DM(xTt[:, c0:c1, :], xT_own[c0:c1, :, tt * TT:(tt + 1) * TT].rearrange("c p t -> p c t")),
                  t_xg[g], writes=t_x[c0:c1])

        def load_at(tt, g):
            c0, c1 = g * GX, (g + 1) * GX
            P.dma(SP, DM(actT[:, c0:c1, :], at_s[c0:c1, :, tt * TT:(tt + 1) * TT].rearrange("c p t -> p c t")),
                  t_ag[g], reads=t_at_s[c0:c1], writes=t_act[c0:c1])

        for tt in range(2):
            if tt == 0:
                for g in range(ND // GX):
                    load_at(tt, g)
                for g in range(ND // GX):
                    load_x(tt, g)

            def wout_body(oc, slab, t_slab):
                b = g_r.next()
                for kc in range(ND):
                    P.op(PE, MM(banks[b][:, :], slab[:, kc * 128:(kc + 1) * 128], actT[:, kc, :],
                                start=(kc == 0), stop=(kc == ND - 1)), reads=[t_slab, t_act[kc]], writes=[tbank[b]])
                acc_x(b, oc)
            run_slabs(slabsB, [(w_out_r[oc], ND * 128) for oc in range(ND)], wout_body)

            norm_to_act(G_CROSS)

            def cq_body(i, slab, t_slab):
                b = g_r.next()
                for c in range(ND):
                    P.op(PE, MM(banks[b][:, :], slab[:, c * 128:(c + 1) * 128], actT[:, c, :],
                                start=(c == 0), stop=(c == ND - 1)), reads=[t_slab, t_act[c]], writes=[tbank[b]])
                P.op(ACT, AC(cqT[:, i, :], banks[b][:, :], AF.Copy), reads=[tbank[b]], writes=[t_cq[i]])
            run_slabs(slabsB, [(w_cq_r[i], ND * 128) for i in range(NCH)], cq_body)

            for hh in range(NCH):
                for mb in range(2):
                    sb = g_r.next()
                    P.op(PE, MM(banks[sb][:, :], ckT[:, hh, mb * 128:(mb + 1) * 128], cqT[:, hh, :]),
                         reads=[t_ck[hh], t_cq[hh]], writes=[tbank[sb]])
                    pt, t_pt = pb_r.next()
                    P.op(ACT, AC(pt, banks[sb][:, :], AF.Exp, scale=scale), reads=[tbank[sb]], writes=[t_pt])
                    P.op(PE, MM(banks[OBk][:, :], cv[:, mb, hh * 128:(hh + 1) * 128], pt, start=(mb == 0), stop=(mb == 1)),
                         reads=[t_cv[mb], t_pt], writes=[tbank[OBk]])
                    P.op(PE, MM(banks[LBk][:, :], ONES, pt, start=(mb == 0), stop=(mb == 1)),
                         reads=[t_pt, t_cb], writes=[tbank[LBk]])
                P.op(DVE, RC(rlB, banks[LBk][:, :]), reads=[tbank[LBk]], writes=[t_rlB])
                P.op(DVE, TTO(coT[:, hh, :], banks[OBk][:, :], rlB, ALU.mult), reads=[tbank[OBk], t_rlB], writes=[t_co[hh]])

            def wco_body(g, slab, t_slab):
                sl = slab[:, 0:NCH * GC].rearrange("p (k n) -> p k n", k=NCH)
                for j in range(GC // 128):
                    oc = g * (GC // 128) + j
                    b = g_r.next()
                    for kc in range(NCH):
                        P.op(PE, MM(banks[b][:, :], sl[:, kc, j * 128:(j + 1) * 128], coT[:, kc, :],
                                    start=(kc == 0), stop=(kc == NCH - 1)), reads=[t_slab, t_co[kc]], writes=[tbank[b]])
                    acc_x(b, oc)
            run_slabs(slabsB, [(w_co_r[g], NCH * GC) for g in range(NGC)], wco_body)

            norm_to_act(G_MLP)
            srcs = []
            order = []
            for s in range(NSLAB + 1):
                if s < NSLAB:
                    for hcl in range(KS):
                        order.append(("up", s, hcl))
                        srcs.append((w_up_r[s * KS + hcl], ND * 128))
                if s >= 1:
                    for og in range(NOG):
                        order.append(("dn", s - 1, og))
                        srcs.append((w_dn_r[(s - 1) * NOG + og], KS * OG))

            def mlp_body(i, slab, t_slab):
                kind, s, j = order[i]
                u_ap, t_u = uT[s % 2]
                if kind == "up":
                    b = g_r.next()
                    for c in range(ND):
                        P.op(PE, MM(banks[b][:, :], slab[:, c * 128:(c + 1) * 128], actT[:, c, :],
                                    start=(c == 0), stop=(c == ND - 1)), reads=[t_slab, t_act[c]], writes=[tbank[b]])
                    tm, t_tm = tmp_r.next()
                    P.op(ACT, AC(tm, banks[b][:, :], AF.Relu), reads=[tbank[b]], writes=[t_tm])
                    P.op(DVE, TTO(u_ap[:, j, :], tm, tm, ALU.mult), reads=[t_tm], writes=[t_u[j]])
                else:
                    sl = slab[:, 0:KS * OG].rearrange("p (k n) -> p k n", k=KS)
                    for jj in range(OG // 128):
                        oc = j * (OG // 128) + jj
                        b = g_r.next()
                        for kc in range(KS):
                            P.op(PE, MM(banks[b][:, :], sl[:, kc, jj * 128:(jj + 1) * 128], u_ap[:, kc, :],
                                        start=(kc == 0), stop=(kc == KS - 1)), reads=[t_slab, t_u[kc]], writes=[tbank[b]])
                        acc_x(b, oc)
            run_slabs(slabsB, srcs, mlp_body)

            if tt + 1 < 2:
                for g in range(ND // GX):
                    load_at(tt + 1, g)
            stats_x()
            for c in range(ND):
                ys, t_ys = tmp_r.next()
                P.op(DVE, STT(ys, xTt[:, c, :], GV[:, G_FIN + c:G_FIN + c + 1], rtB, ALU.mult, ALU.mult),
                     reads=[t_x[c], t_rtB, t_gv], writes=[t_ys])
                P.dma(SP, DM(yT[c][:, tt * TT:(tt + 1) * TT], ys), t_ys, reads=[t_ys], writes=[T("y")])
                if tt + 1 < 2 and (c + 1) % GX == 0:
                    load_x(tt + 1, c // GX)
        P.fence()
        P.emit(nc)
    return nc


def _slabs_cols(w, ND, ncols_chunk=128):
    K, N = w.shape
    a = w.reshape(K // 128, 128, N // 128, 128)
    return np.ascontiguousarray(a.transpose(2, 1, 0, 3)).reshape(N // 128, 128, (K // 128) * 128)


def _slabs_rows(w, kper, ncol):
    K, N = w.shape
    ns = K // 128 // kper
    ng = N // ncol
    a = w.reshape(ns, kper, 128, ng, ncol)
    return np.ascontiguousarray(a.transpose(0, 3, 2, 1, 4)).reshape(ns * ng, 128, kper * ncol)


def _consts():
    ki = np.arange(128)[:, None]
    j = np.arange(19 * 128)[None, :]
    d = j - 384 - ki
    c = ((d >= 0) & (d <= 128)).astype(np.float32) + ((d >= 0) & (d <= 512) & (d % 4 == 0)) + \
        ((d >= 0) & (d <= 2048) & (d % 16 == 0))
    q = np.arange(128)[None, :]
    tri = (q >= ki).astype(np.float32)
    perm = np.zeros((128, 128), np.float32)
    perm[(np.arange(128) + 64) % 128, np.arange(128)] = 1.0
    ones = np.ones((128, 128), np.float32)
    ident = np.eye(128, dtype=np.float32)
    return np.concatenate([c, tri, perm, ones, ident], axis=1).astype(ml_dtypes.bfloat16)


def _rope_tables(pos):
    inv_freq = (10000.0 ** (-np.arange(0, HD, 2, dtype=np.float32) / HD)).astype(np.float32)
    ang = pos.astype(np.float32)[:, None] * inv_freq[None, :]
    cos = np.cos(ang).astype(np.float32)
    sin = np.sin(ang).astype(np.float32)
    cosT = np.concatenate([cos, cos], axis=1).T
    sinT = np.concatenate([-sin, sin], axis=1).T
    return np.ascontiguousarray(cosT), np.ascontiguousarray(sinT)


_PROG_CACHE = {}
_STOP = 99


def _run(inp, D, B):
    ND = D // 128
    f32 = np.float32
    x = np.asarray(inp["x"], f32)
    mem = np.asarray(inp["mem"], f32)
    KS = min(8, ND)
    OG = min(512, D)
    GC = min(1024, D)
    CVK = min(8, ND)
    w_in = np.asarray(inp["w_in"], f32)[0]
    w_ckv = np.asarray(inp["w_ckv"], f32)[0]
    shared = {
        "w_in_r": _slabs_cols(w_in, ND),
        "w_out_r": _slabs_cols(np.asarray(inp["w_out"], f32)[0], ND),
        "w_cq_r": _slabs_cols(np.asarray(inp["w_cq"], f32)[0], ND),
        "w_ck_r": _slabs_cols(np.ascontiguousarray(w_ckv[:, :512]), ND),
        "w_cv_r": _slabs_rows(np.ascontiguousarray(w_ckv[:, 512:]), CVK, 512),
        "w_co_r": _slabs_rows(np.asarray(inp["w_co"], f32)[0], NCH, GC),
        "w_up_r": _slabs_cols(np.asarray(inp["w_up"], f32)[0], ND),
        "w_dn_r": _slabs_rows(np.asarray(inp["w_down"], f32)[0], KS, OG),
        "cb": _consts(),
    }
    gv = np.concatenate([
        np.asarray(inp["norm_mix"], f32)[0].reshape(ND, 128).T,
        np.asarray(inp["norm_cross"], f32)[0].reshape(ND, 128).T,
        np.asarray(inp["norm_mem"], f32)[0].reshape(ND, 128).T,
        np.asarray(inp["norm_mlp"], f32)[0].reshape(ND, 128).T,
        np.asarray(inp["norm_final"], f32).reshape(ND, 128).T,
        np.asarray(inp["diff_subln"], f32)[0].reshape(2, 128).T,
    ], axis=1)
    shared["gv"] = np.ascontiguousarray(gv)
    shared["lp"] = np.ascontiguousarray(np.broadcast_to(np.asarray(inp["diff_lambda"], f32)[0].reshape(1, 512), (128, 512)))
    in_maps = []
    for core in range(2 * B):
        b, qh = core // 2, core % 2
        own = slice(qh * HALF, (qh + 1) * HALF)
        oth = slice((1 - qh) * HALF, (2 - qh) * HALF)
        m = dict(shared)
        m["xT_own"] = np.ascontiguousarray(x[b, own].T).reshape(ND, 128, HALF)
        m["xT_oth"] = np.ascontiguousarray(x[b, oth].T).reshape(ND, 128, HALF)
        m["memT"] = np.ascontiguousarray(mem[b].T).reshape(ND, 128, NMEM)
        m["cos_own"], m["sin_own"] = _rope_tables(np.arange(own.start, own.stop))
        m["cos_oth"], m["sin_oth"] = _rope_tables(np.arange(oth.start, oth.stop))
        m["visb"] = np.full((128, 1), 0.0 if qh == 1 else NEG, f32)
        in_maps.append(m)
    if D not in _PROG_CACHE:
        _PROG_CACHE[D] = build_program(D, _STOP)
    nc = _PROG_CACHE[D]
    res = run_bass_kernel_spmd(nc, in_maps, core_ids=list(range(2 * B)))
    out = np.empty((B, S, D), f32)
    for core in range(2 * B):
        b, qh = core // 2, core % 2
        out[b, qh * HALF:(qh + 1) * HALF] = np.asarray(res.results[core]["yT"]).reshape(D, HALF).T
    return out


def kernel(**inputs):
    return _run(inputs, 4096, 4)
```
